# Optimizing a Trainium2 kernel written in Bass

```python
import jax, jax.numpy as jnp
from jax import lax
import numpy as np

D_MODEL = 2048
BATCH = 8
SEQ = 4096
DEPTH = 2

N_A_LAYERS = DEPTH // 2
N_B_LAYERS = DEPTH - N_A_LAYERS
A_HEADS = 16
A_KEY_DIM = D_MODEL // A_HEADS
A_VAL_DIM = D_MODEL // A_HEADS
A_CHUNK = 16
B_HEADS = 16
B_KV_HEADS = 4
B_HEAD_DIM = D_MODEL // B_HEADS
B_GROUP = B_HEADS // B_KV_HEADS
Q_BLOCK = 128
D_FF = 4 * D_MODEL
NORM_EPS = 1e-6

kernel_name = 'yoco_hgrn2_stickbreaking_adaln_block'


def rms_norm(x, gain):
    xf = x.astype(jnp.float32)
    inv = lax.rsqrt(jnp.mean(xf * xf, axis=-1, keepdims=True) + NORM_EPS)
    return (xf * inv).astype(x.dtype) * gain


def modulate(h, shift, scale):
    return h * (1 + scale[:, None, :]) + shift[:, None, :]


def squared_relu_mlp(h, w1, w2):
    a = jax.nn.relu(h @ w1)
    return (a * a) @ w2


def hgrn2_mixer(h, w_in, lb, out_gain, w_out):
    b, s, _ = h.shape
    proj = h @ w_in
    q, f_logit, i_in, g = jnp.split(proj, 4, axis=-1)
    q = jax.nn.silu(q.astype(jnp.float32))
    f_logit = f_logit.astype(jnp.float32)
    f = lb + (1 - lb) * jax.nn.sigmoid(f_logit)
    log_f = jnp.log(f)
    k = (1 - lb) * jax.nn.sigmoid(-f_logit)
    v = i_in.astype(jnp.float32)
    nc = s // A_CHUNK
    shp = (b, nc, A_CHUNK, A_HEADS, A_KEY_DIM)
    q = q.reshape(shp)
    k = k.reshape(shp)
    log_f = log_f.reshape(shp)
    v = v.reshape(b, nc, A_CHUNK, A_HEADS, A_VAL_DIM)
    cum = jnp.cumsum(log_f, axis=2)
    q_dec = q * jnp.exp(cum)
    k_intra = k * jnp.exp(-cum)
    k_state = k * jnp.exp(cum[:, :, -1:] - cum)
    chunk_decay = jnp.exp(cum[:, :, -1])
    causal = jnp.tril(jnp.ones((A_CHUNK, A_CHUNK), dtype=bool))
    scores = jnp.einsum('bnthk,bnshk->bnhts', q_dec, k_intra)
    scores = jnp.where(causal, scores, 0.0)
    o_intra = jnp.einsum('bnhts,bnshv->bnthv', scores, v)

    def step(state, inp):
        q_c, k_c, v_c, dec_c = inp
        o_c = jnp.einsum('bthk,bhkv->bthv', q_c, state)
        state = dec_c[..., None] * state + jnp.einsum('bthk,bthv->bhkv', k_c, v_c)
        return state, o_c

    xs = (jnp.moveaxis(q_dec, 1, 0), jnp.moveaxis(k_state, 1, 0),
          jnp.moveaxis(v, 1, 0), jnp.moveaxis(chunk_decay, 1, 0))
    s0 = jnp.zeros((b, A_HEADS, A_KEY_DIM, A_VAL_DIM), jnp.float32)
    _, o_inter = lax.scan(step, s0, xs)
    o = (o_intra + jnp.moveaxis(o_inter, 0, 1)).reshape(b, s, A_HEADS, A_VAL_DIM)
    o = o * lax.rsqrt(jnp.mean(o * o, axis=-1, keepdims=True) + NORM_EPS)
    o = o.reshape(b, s, A_HEADS * A_VAL_DIM) * out_gain
    o = o * jax.nn.silu(g.astype(jnp.float32))
    return o.astype(h.dtype) @ w_out


def stick_breaking_mixer(h, w_q, k, v, w_out):
    b, s, _ = h.shape
    q = (h @ w_q).reshape(b, s, B_KV_HEADS, B_GROUP, B_HEAD_DIM)
    scale = B_HEAD_DIM ** -0.5
    outs = []
    for blk in range(s // Q_BLOCK):
        t0 = blk * Q_BLOCK
        t1 = t0 + Q_BLOCK
        q_blk = q[:, t0:t1]
        k_pre = k[:, :t1]
        v_pre = v[:, :t1]
        z = jnp.einsum('btkgd,bskd->bkgts', q_blk, k_pre).astype(jnp.float32) * scale
        t_pos = t0 + jnp.arange(Q_BLOCK)[:, None]
        s_pos = jnp.arange(t1)[None, :]
        mask = s_pos < t_pos
        log_beta = jax.nn.log_sigmoid(z)
        log_rest = jnp.where(mask, log_beta - z, 0.0)
        between = lax.cumsum(log_rest, axis=4, reverse=True) - log_rest
        weights = jnp.where(mask, jnp.exp(log_beta + between), 0.0)
        outs.append(jnp.einsum('bkgts,bskd->btkgd', weights.astype(v.dtype), v_pre))
    o = jnp.concatenate(outs, axis=1).reshape(b, s, B_HEADS * B_HEAD_DIM)
    return o @ w_out


def setup_inputs(seed: int = 0) -> dict:
    key = jax.random.key(seed)
    ks = jax.random.split(key, 20)
    d = D_MODEL
    f32 = jnp.float32

    def nrm(k, shape, fan_in, gain=1.0):
        return jax.random.normal(k, shape, f32) * (gain * fan_in ** -0.5)

    def small(k, shape):
        return 0.02 * jax.random.normal(k, shape, f32)

    return {
        'x': jax.random.normal(ks[0], (BATCH, SEQ, d), f32),
        'c': jax.random.normal(ks[1], (BATCH, d), f32),
        'ada_w': nrm(ks[2], (DEPTH, d, 6 * d), d, 0.5),
        'ada_b': small(ks[3], (DEPTH, 6 * d)),
        'norm_mix': 1.0 + small(ks[4], (DEPTH, d)),
        'norm_mlp': 1.0 + small(ks[5], (DEPTH, d)),
        'a_w_in': nrm(ks[6], (N_A_LAYERS, d, 4 * d), d),
        'a_lb_logits': 0.1 * jax.random.normal(ks[7], (N_A_LAYERS + 1, A_HEADS * A_KEY_DIM), f32),
        'a_out_gain': 1.0 + small(ks[8], (N_A_LAYERS, A_HEADS * A_VAL_DIM)),
        'a_w_out': nrm(ks[9], (N_A_LAYERS, A_HEADS * A_VAL_DIM, d), A_HEADS * A_VAL_DIM),
        'kv_ada_w': nrm(ks[10], (d, 2 * d), d, 0.5),
        'kv_ada_b': small(ks[11], (2 * d,)),
        'kv_norm': 1.0 + small(ks[12], (d,)),
        'w_kv': nrm(ks[13], (d, 2 * B_KV_HEADS * B_HEAD_DIM), d),
        'b_w_q': nrm(ks[14], (N_B_LAYERS, d, B_HEADS * B_HEAD_DIM), d),
        'b_w_out': nrm(ks[15], (N_B_LAYERS, B_HEADS * B_HEAD_DIM, d), B_HEADS * B_HEAD_DIM),
        'mlp_w1': nrm(ks[16], (DEPTH, d, D_FF), d),
        'mlp_w2': nrm(ks[17], (DEPTH, D_FF, d), D_FF),
        'final_norm': 1.0 + small(ks[18], (d,)),
    }


def reference(x, c, ada_w, ada_b, norm_mix, norm_mlp, a_w_in, a_lb_logits, a_out_gain, a_w_out,
              kv_ada_w, kv_ada_b, kv_norm, w_kv, b_w_q, b_w_out, mlp_w1, mlp_w2, final_norm):
    b, s, _ = x.shape
    c_act = jax.nn.silu(c)
    lb_all = jnp.cumsum(jax.nn.softmax(a_lb_logits.astype(jnp.float32), axis=0), axis=0)
    k_shared = None
    v_shared = None
    for layer in range(DEPTH):
        mod = c_act @ ada_w[layer] + ada_b[layer]
        sh1, sc1, g1, sh2, sc2, g2 = jnp.split(mod, 6, axis=-1)
        h = modulate(rms_norm(x, norm_mix[layer]), sh1, sc1)
        if layer < N_A_LAYERS:
            y = hgrn2_mixer(h, a_w_in[layer], lb_all[layer], a_out_gain[layer], a_w_out[layer])
        else:
            j = layer - N_A_LAYERS
            y = stick_breaking_mixer(h, b_w_q[j], k_shared, v_shared, b_w_out[j])
        x = x + g1[:, None, :] * y
        h = modulate(rms_norm(x, norm_mlp[layer]), sh2, sc2)
        x = x + g2[:, None, :] * squared_relu_mlp(h, mlp_w1[layer], mlp_w2[layer])
        if layer == N_A_LAYERS - 1:
            kv_sh, kv_sc = jnp.split(c_act @ kv_ada_w + kv_ada_b, 2, axis=-1)
            hk = modulate(rms_norm(x, kv_norm), kv_sh, kv_sc)
            kv = (hk @ w_kv).reshape(b, s, 2, B_KV_HEADS, B_HEAD_DIM)
            k_shared = kv[:, :, 0]
            v_shared = kv[:, :, 1]
    return rms_norm(x, final_norm)
```

```python
from contextlib import ExitStack
import numpy as np
import concourse.bass as bass
import concourse.mybir as mybir
from concourse.bass_utils import run_bass_kernel_spmd

F32 = mybir.dt.float32
BF16 = mybir.dt.bfloat16
AF = mybir.ActivationFunctionType
ALU = mybir.AluOpType

D = 2048
S = 4096
T = 512
KC = 16
DFF = 8192
EPS = 1e-6
NVEC = 384
NB = 3
PF = 2
POOL_KC = (1, 3, 6, 8, 11, 13)


class Eng:
    def __init__(self, name, handle, sem, is_pe=False):
        self.name = name
        self.h = handle
        self.sem = sem
        self.count = 0
        self.waited = {}
        self.is_pe = is_pe


class Buf:
    __slots__ = ("w", "r", "name")

    def __init__(self, name=""):
        self.w = None
        self.r = {}
        self.name = name


class Sched:
    def __init__(self, nc, stack):
        self.nc = nc
        self.stack = stack
        self.dry = False
        self.nsem = 0
        self.pe = Eng("pe", nc.tensor, self.sem("pe"), True)
        self.act = Eng("act", nc.scalar, self.sem("act"))
        self.dve = Eng("dve", nc.vector, self.sem("dve"))
        self.pool = Eng("pool", nc.gpsimd, self.sem("pool"))
        self.sp = Eng("sp", nc.sync, self.sem("sp"))
        self.regions = []

    def sem(self, name):
        self.nsem += 1
        return self.stack.enter_context(self.nc.semaphore("sem_" + name))

    def veng(self, name):
        return Eng(name, None, self.sem(name))

    def _deps(self, reads, writes):
        deps = {}

        def add(e, v):
            if deps.get(e, 0) < v:
                deps[e] = v
        for b in reads:
            if b.w is not None:
                add(*b.w)
        for b in writes:
            if b.w is not None:
                add(*b.w)
            for e, v in b.r.items():
                add(e, v)
        return deps

    def _wait(self, eng, deps):
        for e, v in deps.items():
            if e is eng and eng.is_pe:
                continue
            if eng.waited.get(e, 0) >= v:
                continue
            eng.h.wait_ge(e.sem, v)
            eng.waited[e] = v

    def op(self, eng, fn, reads=(), writes=()):
        if self.dry:
            return
        self._wait(eng, self._deps(reads, writes))
        inst = fn()
        eng.count += 1
        inst.then_inc(eng.sem, 1)
        for b in reads:
            b.r[eng] = eng.count
        for b in writes:
            b.w = (eng, eng.count)
            b.r = {}

    def dma(self, q, ve, out, in_, reads=(), writes=()):
        if self.dry:
            return
        deps = self._deps(reads, writes)
        if ve.count > 0 and deps.get(ve, 0) < ve.count:
            deps[ve] = ve.count
        self._wait(q, deps)
        inst = q.h.dma_start(out=out, in_=in_)
        ve.count += 16
        inst.then_inc(ve.sem, 16)
        for b in reads:
            b.r[ve] = ve.count
        for b in writes:
            b.w = (ve, ve.count)
            b.r = {}

    def region(self, off, size, name=""):
        nb = Buf(name)
        if self.dry:
            return nb
        keep = []
        for (o, s, b) in self.regions:
            if o < off + size and off < o + s:
                if b.w is not None:
                    e, v = b.w
                    if nb.r.get(e, 0) < v:
                        nb.r[e] = v
                for e, v in b.r.items():
                    if nb.r.get(e, 0) < v:
                        nb.r[e] = v
                if o >= off and o + s <= off + size:
                    continue
            keep.append((o, s, b))
        keep.append((off, size, nb))
        self.regions = keep
        return nb


class WStream:
    def __init__(self, sched, slots, slotbufs, vengs, specs):
        self.S = sched
        self.slots = slots
        self.bufs = slotbufs
        self.vengs = vengs
        self.specs = specs
        self.i = 0
        self.issued = 0
        self.fences = []

    def fence(self):
        if self.S.dry:
            self.fences.append(len(self.specs))

    def next(self, spec):
        S = self.S
        i = self.i
        self.i += 1
        if S.dry:
            self.specs.append(spec)
            return self.slots[i % NB], self.bufs[i % NB]
        lim = min(i + PF, len(self.specs) - 1)
        for f in self.fences:
            if i < f:
                lim = min(lim, f - 1)
                break
        while self.issued <= lim:
            j = self.issued
            sp_ = self.specs[j](self.slots[j % NB])
            q, out_view, in_ap, rbufs = sp_[:4]
            S.dma(q, self.vengs[j % NB], out_view, in_ap, reads=rbufs, writes=[self.bufs[j % NB]])
            if len(sp_) > 4:
                q2, dst_ap, dst_bufs = sp_[4:]
                S.dma(q2, self.svengs[j % NB], dst_ap, self.slots[j % NB][:], reads=[self.bufs[j % NB]], writes=dst_bufs)
            self.issued += 1
        return self.slots[i % NB], self.bufs[i % NB]


_WLIST = []


class _Stop(Exception):
    pass


def build(ntiles=S // T, dbg=False, stage=99):
    nc = bass.Bass("TRN2", target_bir_lowering=False)
    NTOK = S

    def din(name, shape, dt=F32):
        return nc.dram_tensor(name, shape, dt, kind="ExternalInput").ap()

    x_d = din("x", [S, D])
    vec_d = din("vecs", [128, NVEC])
    wt32_d = din("wt32", [94, 128, 8192])
    ada32_d = din("ada32", [56, 128, 8192])
    adaw_d = ["ada_w0", "ada_w1"]
    kvada_d = "kv_ada_w"
    win_d, wao_d, wkv_d, wq_d, wbo_d = "w_in", "a_w_out", "w_kv", "b_w_q", "b_w_out"
    w1_d = ["w1_0", "w1_1"]
    w2_d = ["w2_0", "w2_1"]
    y_d = nc.dram_tensor("y", [S, D], F32, kind="ExternalOutput").ap()
    NW = 94
    Wd = nc.dram_tensor("Wd", [NW, 128, 8192], BF16, kind="Internal").ap()
    Kd = nc.dram_tensor("Kd", [4, 128, S], BF16, kind="Internal").ap()
    Vd = nc.dram_tensor("Vd", [4, 128, S // 128, 128], BF16, kind="Internal").ap()

    with ExitStack() as st:
        S_ = Sched(nc, st)
        PE, ACT, DVE, POOL, SP = S_.pe, S_.act, S_.dve, S_.pool, S_.sp

        def sb(name, shape, dt):
            return st.enter_context(nc.sbuf_tensor(name, shape, dt))

        xT = sb("xT", [128, KC, T], F32)
        hT = sb("hT", [128, KC, T], BF16)
        oT = sb("oT", [128, KC, T], BF16)
        wsl = [sb("wslot%d" % i, [128, 8192], BF16) for i in range(NB)]
        st32 = sb("st32", [128, 16, 128], F32)
        st16 = sb("st16", [128, 16, 128], BF16)
        vecs = sb("vecs_sb", [128, NVEC], F32)
        mod = [sb("mod0", [128, 96], F32), sb("mod1", [128, 96], F32)]
        modkv = sb("modkv", [128, 32], F32)
        gv = sb("gv", [128, 5, 16], F32)
        lbv = sb("lbv", [128, 3, 16], F32)
        cact = sb("cact", [128, 16], BF16)
        ones32 = sb("ones32", [128, 128], F32)
        ident32 = sb("ident32", [128, 128], F32)
        identb = sb("identb", [128, 128], BF16)
        onesb = sb("onesb", [128, 128], BF16)
        negones = sb("negones", [128, 128], BF16)
        negtri = sb("negtri", [128, 128], BF16)
        mask_h = sb("mask_h", [128, 128], F32)
        mask_s = sb("mask_s", [128, 128], BF16)
        resetm = sb("resetm", [128, T], F32)
        negbig = sb("negbig", [128, 128], BF16)
        mneg = sb("mneg", [128, 4, 128], BF16)
        ARENA = 32768
        arena = sb("arena", [128, ARENA], BF16)

        banks = [st.enter_context(nc.psum_tensor("bank%d" % i, [128, 512], F32)) for i in range(8)]
        bankb = [[Buf("bank%d_%d" % (i, j)) for j in range(4)] for i in range(8)]
        rot = {"A": 0, "B": 0, "Q": 0}

        def bankA():
            i = rot["A"] % 4
            rot["A"] += 1
            return banks[i], bankb[i]

        def bankB():
            i = 4 + rot["B"] % 4
            rot["B"] += 1
            return banks[i], bankb[i]

        def quarterB():
            bk, bb = bankB()
            return bk[:, 0:128], bb

        def a16(off, n):
            return arena[:, off:off + n]

        def a32(off, n):
            return arena[:, off:off + 2 * n].bitcast(F32)

        def reg(off, n, name=""):
            return S_.region(off, n, name)

        xT_b = [Buf("xT%d" % k) for k in range(KC)]
        hT_b = [Buf("hT%d" % k) for k in range(KC)]
        oT_b = [Buf("oT%d" % k) for k in range(KC)]
        st32_b = [Buf("st32_%d" % k) for k in range(16)]
        st16_b = [Buf("st16_%d" % k) for k in range(16)]
        wbufs = [Buf("wslot%d" % i) for i in range(NB)]
        wvengs = [S_.veng("wv%d" % i) for i in range(NB)]
        cbuf = Buf("consts")
        vbuf = Buf("vecs")
        Wd_b = [Buf("Wd%d" % i) for i in range(94)]
        NCV = 8
        cast_slot_b = [Buf("cslot%d" % i) for i in range(NCV)]
        Kd_b = [Buf("Kd%d" % i) for i in range(4)]
        Vd_b = [Buf("Vd%d" % i) for i in range(4)]
        castv = [S_.veng("cast%d" % i) for i in range(8)]
        kst = [S_.veng("kst%d" % i) for i in range(4)]
        vst = [S_.veng("vst%d" % i) for i in range(4)]
        ldv = [S_.veng("ld0"), S_.veng("ld1")]
        kgv = [S_.veng("kg0"), S_.veng("kg1")]
        vgv = [S_.veng("vg0"), S_.veng("vg1")]
        outv = [S_.veng("outv0"), S_.veng("outv1")]

        def A(fn, reads=(), writes=()):
            S_.op(ACT, fn, reads, writes)

        def V(fn, reads=(), writes=()):
            S_.op(DVE, fn, reads, writes)

        def P(fn, reads=(), writes=()):
            S_.op(POOL, fn, reads, writes)

        def M(fn, reads=(), writes=()):
            S_.op(PE, fn, reads, writes)

        def act(out, in_, func, bias=None, scale=None):
            kw = {}
            if bias is not None:
                kw["bias"] = bias
            if scale is not None:
                kw["scale"] = scale
            return lambda: nc.scalar.activation(out=out, in_=in_, func=func, **kw)

        cur_ti = [0]

        def specA(idx):
            first = (cur_ti[0] == 0)

            def f(slot):
                if not first:
                    return SP, slot[:], Wd[idx], [Wd_b[idx]]
                ov = slot.rearrange("p (a e) -> p a e", e=2048)
                iv = wt32_d[idx].rearrange("p (a e) -> p a e", e=2048)
                return POOL, ov, iv, [], SP, Wd[idx], [Wd_b[idx]]
            return f

        ADA_BASE = {"ada_w0": 0, "ada_w1": 24, "kv_ada_w": 48}

        def specAda(src, col0):
            ai = ADA_BASE[src] + col0 // 512

            def f(slot):
                return (POOL, slot.rearrange("p (a e) -> p a e", e=2048),
                        ada32_d[ai].rearrange("p (a e) -> p a e", e=2048), [])
            return f

        wlist = []

        def addA(src, col0):
            wlist.append(("A", src, col0))
            return len(wlist) - 1

        def addB(src, a, q):
            wlist.append(("B", src, (a, q)))
            return len(wlist) - 1

        widx = {}
        for hg in range(4):
            for m in range(4):
                widx[("in", hg, m)] = addA(win_d, m * 2048 + hg * 512)
        for j in range(4):
            widx[("ao", j)] = addA(wao_d, j * 512)
        for l in range(2):
            for a in range(2):
                for j in range(8):
                    widx[("w1", l, a, j)] = addA(w1_d[l], a * 4096 + j * 512)
                for q in range(8):
                    widx[("w2", l, a, q)] = addB(w2_d[l], a, q)
        for j in range(2):
            widx[("kv", j)] = addA(wkv_d, j * 512)
        for j in range(4):
            widx[("q", j)] = addA(wq_d, j * 512)
        for j in range(4):
            widx[("bo", j)] = addA(wbo_d, j * 512)
        assert len(wlist) == NW
        _WLIST[:] = wlist

        ws = WStream(S_, wsl, wbufs, wvengs, [])
        ws.svengs = [S_.veng("wsv%d" % i) for i in range(NB)]

        def stg(n):
            if stage < n:
                raise _Stop()

        def program():
            try:
                program_()
            except _Stop:
                pass

        def program_():
            rot["A"] = rot["B"] = rot["Q"] = 0
            ws.i = 0
            ws.issued = 0
            P(lambda: nc.gpsimd.memset(ones32[:], 1.0), writes=[cbuf])
            P(lambda: nc.gpsimd.memset(onesb[:], 1.0), writes=[cbuf])
            P(lambda: nc.gpsimd.memset(negones[:], -1.0), writes=[cbuf])
            P(lambda: nc.gpsimd.memset(resetm[:], 1.0), writes=[cbuf])
            P(lambda: nc.gpsimd.memset(resetm[:].rearrange("p (c t) -> p c t", t=64)[:, :, 0:1], 0.0),
              reads=[cbuf], writes=[cbuf])
            P(lambda: nc.gpsimd.memset(st32[:], 0.0), writes=st32_b)
            P(lambda: nc.gpsimd.memset(st16[:], 0.0), writes=st16_b)
            P(lambda: nc.gpsimd.affine_select(out=ident32[:], in_=ones32[:], pattern=[[1, 128]],
                                              compare_op=ALU.is_equal, fill=0.0, base=0, channel_multiplier=-1),
              reads=[cbuf], writes=[cbuf])
            P(lambda: nc.gpsimd.tensor_copy(out=identb[:], in_=ident32[:]), reads=[cbuf], writes=[cbuf])
            P(lambda: nc.gpsimd.affine_select(out=mask_h[:], in_=ones32[:], pattern=[[1, 128]],
                                              compare_op=ALU.is_ge, fill=0.0, base=0, channel_multiplier=-1),
              reads=[cbuf], writes=[cbuf])
            P(lambda: nc.gpsimd.memset(mask_h[0:64, 64:128], 0.0), reads=[cbuf], writes=[cbuf])
            P(lambda: nc.gpsimd.affine_select(out=mask_s[:], in_=ones32[:], pattern=[[1, 128]],
                                              compare_op=ALU.is_ge, fill=0.0, base=-1, channel_multiplier=-1),
              reads=[cbuf], writes=[cbuf])
            P(lambda: nc.gpsimd.affine_select(out=negtri[:], in_=negones[:], pattern=[[-1, 128]],
                                              compare_op=ALU.is_ge, fill=0.0, base=0, channel_multiplier=1),
              reads=[cbuf], writes=[cbuf])
            P(lambda: nc.gpsimd.memset(negbig[:], -30000.0), writes=[cbuf])
            for h_ in range(4):
                P(lambda h_=h_: nc.gpsimd.affine_select(out=mneg[:, h_, :], in_=negbig[:], pattern=[[-1, 128]],
                                                  compare_op=ALU.is_ge, fill=0.0, base=0, channel_multiplier=1),
                  reads=[cbuf], writes=[cbuf])
            S_.dma(SP, ldv[0], vecs[:], vec_d, writes=[vbuf])
            A(act(cact[:], vecs[:, 0:16], AF.Silu), reads=[vbuf], writes=[cbuf])
            V(lambda: nc.vector.tensor_tensor(out=lbv[:, 0, :], in0=vecs[:, 304:320], in1=vecs[:, 320:336],
                                              op=ALU.subtract), reads=[vbuf], writes=[cbuf])
            A(act(lbv[:, 0, :], lbv[:, 0, :], AF.Sigmoid), reads=[cbuf], writes=[cbuf])
            V(lambda: nc.vector.tensor_scalar(out=lbv[:, 1, :], in0=lbv[:, 0, :], scalar1=-1.0, scalar2=1.0,
                                              op0=ALU.mult, op1=ALU.add), reads=[cbuf], writes=[cbuf])
            V(lambda: nc.vector.tensor_scalar(out=lbv[:, 2, :], in0=lbv[:, 0, :], scalar1=1.0, scalar2=-1.0,
                                              op0=ALU.mult, op1=ALU.add), reads=[cbuf], writes=[cbuf])

            stg(1)
            def ada(src, ncol, dst, bias_cols):
                mps, mpb = banks[4], bankb[4]
                for j in range(ncol // 512):
                    wt, wb = ws.next(specAda(src, j * 512))
                    wv = wt.rearrange("p (kc c) -> p kc c", c=512)
                    for jj in range(4):
                        c = j * 4 + jj
                        for kc in range(KC):
                            M(lambda kc=kc, jj=jj, c=c: nc.tensor.matmul(
                                mps[:, c:c + 1], lhsT=wv[:, kc, jj * 128:(jj + 1) * 128], rhs=cact[:, kc:kc + 1],
                                start=(kc == 0), stop=(kc == KC - 1)), reads=[wb, cbuf], writes=mpb)
                nco = ncol // 128
                V(lambda: nc.vector.tensor_tensor(out=dst[:, 0:nco], in0=mps[:, 0:nco], in1=bias_cols, op=ALU.add),
                  reads=mpb + [vbuf], writes=[cbuf])

            ada(adaw_d[0], 6 * D, mod[0], vecs[:, 16:112])
            ada(adaw_d[1], 6 * D, mod[1], vecs[:, 112:208])
            ada(kvada_d, 2 * D, modkv, vecs[:, 208:240])
            ws.fence()
            gsrc = [(mod[0][:, 16:32], vecs[:, 240:256]), (mod[0][:, 64:80], vecs[:, 272:288]),
                    (modkv[:, 16:32], vecs[:, 352:368]), (mod[1][:, 16:32], vecs[:, 256:272]),
                    (mod[1][:, 64:80], vecs[:, 288:304])]
            for gi, (sc, gn) in enumerate(gsrc):
                V(lambda gi=gi, sc=sc, gn=gn: nc.vector.scalar_tensor_tensor(
                    out=gv[:, gi, :], in0=sc, scalar=1.0, in1=gn, op0=ALU.add, op1=ALU.mult),
                  reads=[cbuf, vbuf], writes=[cbuf])
            shs = [mod[0][:, 0:16], mod[0][:, 48:64], modkv[:, 0:16], mod[1][:, 0:16], mod[1][:, 48:64]]
            gates = [mod[0][:, 32:48], mod[0][:, 80:96], None, mod[1][:, 32:48], mod[1][:, 80:96]]

            stg(2)
            NRM = 20480

            use_pool = [False]

            def rmsnorm(gi, out_fn, out_bufs, direct=None):
                sq = a16(NRM, 8192).rearrange("p (k t) -> p k t", t=T)
                sq_b = [reg(NRM + k * 512, 512, "sq") for k in range(KC)]
                lnb = a32(NRM + 8192, 512)
                rstd = a32(NRM + 9216, 512)
                ln_b = reg(NRM + 8192, 1024, "ln")
                rs_b = reg(NRM + 9216, 1024, "rstd")
                tmp = [a32(NRM + 10240 + i * 1024, 512) for i in range(2)] + [a32(NRM - 2048 + i * 1024, 512) for i in range(2)]
                tmp_b = [reg(NRM + 10240 + i * 1024, 1024, "ntmp") for i in range(2)] + \
                        [reg(NRM - 2048 + i * 1024, 1024, "ntmp") for i in range(2)]
                for kc in range(KC):
                    A(act(sq[:, kc, :], xT[:, kc, :], AF.Square), reads=[xT_b[kc]], writes=[sq_b[kc]])
                ss, ssb = bankB()
                for kc in range(KC):
                    M(lambda kc=kc: nc.tensor.matmul(ss[:], lhsT=onesb[:], rhs=sq[:, kc, :], start=(kc == 0),
                                                     stop=(kc == KC - 1)), reads=[sq_b[kc], cbuf], writes=ssb)
                A(act(lnb, ss[:], AF.Ln, bias=EPS, scale=1.0 / D), reads=ssb, writes=[ln_b])
                A(act(rstd, lnb, AF.Exp, scale=-0.5), reads=[ln_b], writes=[rs_b])
                for kc in range(KC):
                    g_ap = (gv[:, gi, kc:kc + 1] if gi < 5 else vecs[:, 368 + kc:369 + kc])
                    if direct is None and kc % 16 in POOL_KC and use_pool[0]:
                        t_ap, t_b = tmp[2 + (kc % 2)], tmp_b[2 + (kc % 2)]
                        P(lambda kc=kc, t_ap=t_ap: nc.gpsimd.tensor_tensor(out=t_ap, in0=xT[:, kc, :], in1=rstd, op=ALU.mult),
                          reads=[xT_b[kc], rs_b], writes=[t_b])
                        out_fn(kc, t_ap, t_b, g_ap)
                        continue
                    t_ap, t_b = tmp[kc % 2], tmp_b[kc % 2]
                    if direct is not None:
                        t_ap, t_b = direct(kc)
                    V(lambda kc=kc, t_ap=t_ap: nc.vector.scalar_tensor_tensor(
                        out=t_ap, in0=xT[:, kc, :], scalar=g_ap,
                        in1=rstd, op0=ALU.mult, op1=ALU.mult), reads=[xT_b[kc], rs_b, cbuf, vbuf], writes=[t_b])
                    out_fn(kc, t_ap, t_b, None)

            def norm_to_hT(gi):
                def fin(kc, t_ap, t_b, g_ap):
                    A(act(hT[:, kc, :], t_ap, AF.Identity, bias=shs[gi][:, kc:kc + 1], scale=(1.0 if g_ap is None else g_ap)),
                      reads=[t_b, cbuf], writes=[hT_b[kc]])
                rmsnorm(gi, fin, hT_b)

            def norm_dual(gi_a, gi_b):
                sq = a16(NRM, 8192).rearrange("p (k t) -> p k t", t=T)
                sq_b = [reg(NRM + k * 512, 512, "sq") for k in range(KC)]
                lnb = a32(NRM + 8192, 512)
                rstd = a32(NRM + 9216, 512)
                ln_b = reg(NRM + 8192, 1024, "ln")
                rs_b = reg(NRM + 9216, 1024, "rstd")
                T16 = 4096
                t16 = a32(T16, 8192).rearrange("p (k t) -> p k t", t=T)
                t16_b = [reg(T16 + k * 1024, 1024, "t16") for k in range(KC)]
                for kc in range(KC):
                    A(act(sq[:, kc, :], xT[:, kc, :], AF.Square), reads=[xT_b[kc]], writes=[sq_b[kc]])
                ss, ssb = bankB()
                for kc in range(KC):
                    M(lambda kc=kc: nc.tensor.matmul(ss[:], lhsT=onesb[:], rhs=sq[:, kc, :], start=(kc == 0),
                                                     stop=(kc == KC - 1)), reads=[sq_b[kc], cbuf], writes=ssb)
                A(act(lnb, ss[:], AF.Ln, bias=EPS, scale=1.0 / D), reads=ssb, writes=[ln_b])
                A(act(rstd, lnb, AF.Exp, scale=-0.5), reads=[ln_b], writes=[rs_b])
                for kc in range(KC):
                    if kc % 16 in POOL_KC and use_pool[0]:
                        P(lambda kc=kc: nc.gpsimd.tensor_tensor(out=t16[:, kc, :], in0=xT[:, kc, :], in1=rstd, op=ALU.mult),
                          reads=[xT_b[kc], rs_b], writes=[t16_b[kc]])
                    else:
                        V(lambda kc=kc: nc.vector.tensor_tensor(out=t16[:, kc, :], in0=xT[:, kc, :], in1=rstd, op=ALU.mult),
                          reads=[xT_b[kc], rs_b], writes=[t16_b[kc]])
                for gi, dst, dst_b in ((gi_a, hT, hT_b), (gi_b, oT, oT_b)):
                    for kc in range(KC):
                        A(act(dst[:, kc, :], t16[:, kc, :], AF.Identity, bias=shs[gi][:, kc:kc + 1],
                              scale=gv[:, gi, kc:kc + 1]), reads=[t16_b[kc], cbuf], writes=[dst_b[kc]])

            def proj_fm(widx_key, src, src_b, evac, bank_fn=None):
                wt, wb = ws.next(specA(widx[widx_key]))
                wv = wt.rearrange("p (kc c) -> p kc c", c=512)
                for j in range(4):
                    bk, bb = (bank_fn or bankA)()
                    for kc in range(KC):
                        M(lambda kc=kc, j=j, bk=bk: nc.tensor.matmul(
                            bk[:], lhsT=wv[:, kc, j * 128:(j + 1) * 128], rhs=src[:, kc, :],
                            start=(kc == 0), stop=(kc == KC - 1)), reads=[wb, src_b[kc]], writes=bb)
                    evac(j, bk, bb)

            def proj_tm(widx_key, src, src_b, evac):
                wt, wb = ws.next(specA(widx[widx_key]))
                wv = wt.rearrange("p (kc c) -> p kc c", c=512)
                for tb in range(4):
                    bk, bb = bankA()
                    for kc in range(KC):
                        M(lambda kc=kc, tb=tb, bk=bk: nc.tensor.matmul(
                            bk[:], lhsT=src[:, kc, tb * 128:(tb + 1) * 128], rhs=wv[:, kc, :],
                            start=(kc == 0), stop=(kc == KC - 1)), reads=[wb, src_b[kc]], writes=bb)
                    evac(tb, bk, bb)

            def resid_update(widx_keys, gate):
                for j4, key in enumerate(widx_keys):
                    def ev(j, bk, bb, j4=j4):
                        dc = j4 * 4 + j
                        V(lambda: nc.vector.scalar_tensor_tensor(
                            out=xT[:, dc, :], in0=bk[:], scalar=gate[:, dc:dc + 1], in1=xT[:, dc, :],
                            op0=ALU.mult, op1=ALU.add), reads=bb + [xT_b[dc], cbuf], writes=[xT_b[dc]])
                    proj_fm(key, oT, oT_b, ev)

            def mlp(l, gi):
                norm_to_hT(gi)
                gate = gates[gi]
                HID = 0
                hid = a16(HID, 16384).rearrange("p (k t) -> p k t", t=T)
                rl = [a16(16384 + i * 512, 512) for i in range(2)]
                for a in range(2):
                    hid_b = [reg(HID + k * 512, 512, "hid") for k in range(32)]
                    rl_b = [reg(16384 + i * 512, 512, "rl") for i in range(2)]
                    cnt = [0]
                    for j8 in range(8):
                        def ev(j, bk, bb, j8=j8):
                            hc = j8 * 4 + j
                            r_ap, r_b = rl[cnt[0] % 2], rl_b[cnt[0] % 2]
                            cnt[0] += 1
                            A(act(r_ap, bk[:], AF.Relu), reads=bb, writes=[r_b])
                            V(lambda: nc.vector.tensor_tensor(out=hid[:, hc, :], in0=r_ap, in1=r_ap, op=ALU.mult),
                              reads=[r_b], writes=[hid_b[hc]])
                        proj_fm(("w1", l, a, j8), hT, hT_b, ev)
                    for q in range(8):
                        wt, wb = ws.next(specA(widx[("w2", l, a, q)]))
                        wv = wt.rearrange("p (hc c) -> p hc c", c=256)
                        for dd in range(2):
                            dc = q * 2 + dd
                            bk, bb = bankA()
                            for hc in range(32):
                                M(lambda hc=hc, dd=dd, bk=bk: nc.tensor.matmul(
                                    bk[:], lhsT=wv[:, hc, dd * 128:(dd + 1) * 128], rhs=hid[:, hc, :],
                                    start=(hc == 0), stop=(hc == 31)), reads=[wb, hid_b[hc]], writes=bb)
                            V(lambda dc=dc, bk=bk: nc.vector.scalar_tensor_tensor(
                                out=xT[:, dc, :], in0=bk[:], scalar=gate[:, dc:dc + 1], in1=xT[:, dc, :],
                                op0=ALU.mult, op1=ALU.add), reads=bb + [xT_b[dc], cbuf], writes=[xT_b[dc]])

            stg(3)
            for ti in range(ntiles):
                r0 = ti * T
                cur_ti[0] = ti
                use_pool[0] = ti >= 1
                STG = 0
                for tb in range(4):
                    sg = a32(STG + (tb % 2) * 4096, 2048)
                    sg_b = reg(STG + (tb % 2) * 4096, 4096, "stg")
                    S_.dma(SP, ldv[tb % 2], sg, x_d[r0 + tb * 128:r0 + (tb + 1) * 128, :], writes=[sg_b])
                    for q in range(4):
                        bk, bb = bankB()
                        for i in range(4):
                            kc = q * 4 + i
                            M(lambda kc=kc, i=i, bk=bk: nc.tensor.transpose(
                                bk[:, i * 128:(i + 1) * 128], sg[:, kc * 128:(kc + 1) * 128], ident32[:]),
                              reads=[sg_b, cbuf], writes=[bb[i]])
                        A(lambda q=q, tb=tb, bk=bk: nc.scalar.copy(
                            out=xT[:, q * 4:(q + 1) * 4, tb * 128:(tb + 1) * 128],
                            in_=bk[:].rearrange("p (i t) -> p i t", t=128)),
                          reads=bb, writes=[xT_b[q * 4 + i] for i in range(4)])

                stg(4)
                norm_to_hT(0)
                stg(4.1)
                O_SIG, O_LOGF, O_CUM = 0, 4096, 8192
                O_QS, O_QD, O_KI, O_KS, O_KT, O_VT, O_GS = 12288, 14336, 16384, 18432, 20480, 22528, 24576
                O_TMP = 26624
                for hg in range(4):
                    if hg == 0:
                        def v4(off, dt32):
                            if dt32:
                                return a32(off, 2048).rearrange("p (h t) -> p h t", t=T)
                            return a16(off, 2048).rearrange("p (h t) -> p h t", t=T)
                        sig, logf, cum = v4(O_SIG, True), v4(O_LOGF, True), v4(O_CUM, True)
                        qs, qd, ki, ks, gs = v4(O_QS, False), v4(O_QD, False), v4(O_KI, False), v4(O_KS, False), v4(O_GS, False)
                        kt = a16(O_KT, 2048).rearrange("p (tb h k) -> p tb h k", h=4, k=128)
                        vt = a16(O_VT, 2048).rearrange("p (tb c) -> p tb c", c=512)
                        sig_b = [reg(O_SIG + h * 1024, 1024) for h in range(4)]
                        logf_b = [reg(O_LOGF + h * 1024, 1024) for h in range(4)]
                        cum_b = [reg(O_CUM + h * 1024, 1024) for h in range(4)]
                        qs_b = [reg(O_QS + h * 512, 512) for h in range(4)]
                        qd_b = [reg(O_QD + h * 512, 512) for h in range(4)]
                        ki_b = [reg(O_KI + h * 512, 512) for h in range(4)]
                        ks_b = [reg(O_KS + h * 512, 512) for h in range(4)]
                        gs_b = [reg(O_GS + h * 512, 512) for h in range(4)]
                        kt_b = [reg(O_KT + tb * 512, 512) for tb in range(4)]
                        vt_b = [reg(O_VT + tb * 512, 512) for tb in range(4)]
                        sct = [a16(O_TMP + i * 128, 128) for i in range(4)]
                        sct_b = [reg(O_TMP + i * 128, 128) for i in range(4)]
                        osb = [a16(O_TMP + 512 + i * 512, 512) for i in range(4)]
                        osb_b = [reg(O_TMP + 512 + i * 512, 512) for i in range(4)]
                        osq = [a16(O_TMP + 2560 + i * 512, 512) for i in range(4)]
                        osq_b = [reg(O_TMP + 2560 + i * 512, 512) for i in range(4)]
                        oln = a32(O_TMP + 4608, 512)
                        oln_b = reg(O_TMP + 4608, 1024)

                    def ev_f(hl, bk, bb, hg_=None):
                        hd = hg_ * 4 + hl
                        A(act(sig[:, hl, :], bk[:], AF.Sigmoid), reads=bb, writes=[sig_b[hl]])
                        A(act(logf[:, hl, :], sig[:, hl, :], AF.Ln, bias=lbv[:, 0, hd:hd + 1], scale=lbv[:, 1, hd:hd + 1]),
                          reads=[sig_b[hl], cbuf], writes=[logf_b[hl]])
                        V(lambda: nc.vector.tensor_scalar(out=sig[:, hl, :], in0=sig[:, hl, :],
                                                          scalar1=lbv[:, 2, hd:hd + 1], scalar2=lbv[:, 1, hd:hd + 1],
                                                          op0=ALU.mult, op1=ALU.add),
                          reads=[sig_b[hl], cbuf], writes=[sig_b[hl]])
                        V(lambda: nc.vector.tensor_tensor_scan(out=cum[:, hl, :], data0=resetm[:], data1=logf[:, hl, :],
                                                               initial=0.0, op0=ALU.mult, op1=ALU.add),
                          reads=[logf_b[hl], cbuf], writes=[cum_b[hl]])
                        A(act(logf[:, hl, :], cum[:, hl, :], AF.Exp), reads=[cum_b[hl]], writes=[logf_b[hl]])
                        A(act(cum[:, hl, :], cum[:, hl, :], AF.Exp, scale=-1.0), reads=[cum_b[hl]], writes=[cum_b[hl]])
                        V(lambda: nc.vector.tensor_tensor(out=ki[:, hl, :], in0=sig[:, hl, :], in1=cum[:, hl, :], op=ALU.mult),
                          reads=[sig_b[hl], cum_b[hl]], writes=[ki_b[hl]])
                        decb = logf[:, hl, :].rearrange("p (c t) -> p c t", t=64)[:, :, 63:64].broadcast_to([128, 8, 64])
                        V(lambda: nc.vector.tensor_tensor(out=ks[:, hl, :].rearrange("p (c t) -> p c t", t=64),
                                                          in0=ki[:, hl, :].rearrange("p (c t) -> p c t", t=64),
                                                          in1=decb, op=ALU.mult),
                          reads=[ki_b[hl], logf_b[hl]], writes=[ks_b[hl]])
                    def do_f(hg_, bank_fn=None):
                        proj_fm(("in", hg_, 1), hT, hT_b, lambda hl, bk, bb: ev_f(hl, bk, bb, hg_), bank_fn=bank_fn)
                    if hg == 0:
                        do_f(0)
                    stg(4.2)

                    def ev_q(hl, bk, bb):
                        A(act(qs[:, hl, :], bk[:], AF.Sigmoid), reads=bb, writes=[qs_b[hl]])
                        V(lambda: nc.vector.tensor_tensor(out=qs[:, hl, :], in0=bk[:], in1=qs[:, hl, :], op=ALU.mult),
                          reads=bb + [qs_b[hl]], writes=[qs_b[hl]])
                        V(lambda: nc.vector.tensor_tensor(out=qd[:, hl, :], in0=qs[:, hl, :], in1=logf[:, hl, :], op=ALU.mult),
                          reads=[qs_b[hl], logf_b[hl]], writes=[qd_b[hl]])
                    proj_fm(("in", hg, 0), hT, hT_b, ev_q)
                    stg(4.3)

                    stg(4.4)

                    def ev_i(tb, bk, bb):
                        A(lambda: nc.scalar.copy(out=vt[:, tb, :], in_=bk[:]), reads=bb, writes=[vt_b[tb]])
                    proj_tm(("in", hg, 2), hT, hT_b, ev_i)
                    stg(4.45)

                    for tb in range(4):
                        bk, bb = bankB()
                        bkb = bk[:].bitcast(BF16)
                        for hl in range(4):
                            M(lambda hl=hl, tb=tb, bkb=bkb: nc.tensor.transpose(
                                bkb[:, hl * 128:(hl + 1) * 128], ks[:, hl, tb * 128:(tb + 1) * 128], identb[:]),
                              reads=[ks_b[hl], cbuf], writes=bb[0:2])
                        A(lambda tb=tb, bkb=bkb: nc.scalar.copy(
                            out=kt[:, tb, :, :], in_=bkb[:, 0:512].rearrange("p (h k) -> p h k", k=128)),
                          reads=bb[0:2], writes=[kt_b[tb]])


                    def ev_g(hl, bk, bb):
                        if stage == 4.47:
                            return
                        A(act(gs[:, hl, :], bk[:], AF.Sigmoid), reads=bb, writes=[gs_b[hl]])
                        V(lambda: nc.vector.tensor_tensor(out=gs[:, hl, :], in0=bk[:], in1=gs[:, hl, :], op=ALU.mult),
                          reads=bb + [gs_b[hl]], writes=[gs_b[hl]])
                    proj_fm(("in", hg, 3), hT, hT_b, ev_g)
                    stg(4.5)
                    ob = [(banks[i], bankb[i]) for i in range(4)]
                    for tb in range(4):
                        scs = []
                        for hl in range(4):
                            qv, qb_ = quarterB()
                            M(lambda hl=hl, tb=tb, qv=qv: nc.tensor.matmul(
                                qv, lhsT=ki[:, hl, tb * 128:(tb + 1) * 128], rhs=qd[:, hl, tb * 128:(tb + 1) * 128],
                                start=True, stop=True), reads=[ki_b[hl], qd_b[hl]], writes=qb_)
                            s_ap, s_b = sct[hl], sct_b[hl]
                            V(lambda qv=qv, s_ap=s_ap: nc.vector.tensor_tensor(out=s_ap, in0=qv, in1=mask_h[:], op=ALU.mult),
                              reads=qb_ + [cbuf], writes=[s_b])
                            scs.append((s_ap, s_b))
                        for h2 in range(4):
                            s_ap2, s_b2 = scs[h2]
                            obk, obb = ob[h2]
                            M(lambda h2=h2, tb=tb, obk=obk, s_ap2=s_ap2: nc.tensor.matmul(
                                obk[:, tb * 128:(tb + 1) * 128], lhsT=vt[:, tb, h2 * 128:(h2 + 1) * 128], rhs=s_ap2,
                                start=True, stop=False), reads=[vt_b[tb], s_b2], writes=obb)
                        for cc in range(2):
                            c = tb * 2 + cc
                            for h2 in range(4):
                                hd = hg * 4 + h2
                                obk, obb = ob[h2]
                                M(lambda h2=h2, c=c, hd=hd, obk=obk, cc=cc: nc.tensor.matmul(
                                    obk[:, c * 64:(c + 1) * 64], lhsT=st16[:, hd, :], rhs=qd[:, h2, c * 64:(c + 1) * 64],
                                    start=False, stop=(cc == 1)), reads=[st16_b[hd], qd_b[h2]], writes=obb)
                                sv, svb = quarterB()
                                M(lambda h2=h2, tb=tb, cc=cc, sv=sv: nc.tensor.matmul(
                                    sv, lhsT=kt[cc * 64:(cc + 1) * 64, tb, h2, :],
                                    rhs=vt[cc * 64:(cc + 1) * 64, tb, h2 * 128:(h2 + 1) * 128],
                                    start=True, stop=True), reads=[kt_b[tb], vt_b[tb]], writes=svb)
                                dec = logf[:, h2, c * 64 + 63:c * 64 + 64]
                                V(lambda hd=hd, sv=sv, dec=dec: nc.vector.scalar_tensor_tensor(
                                    out=st16[:, hd, :], in0=st32[:, hd, :], scalar=dec,
                                    in1=sv, op0=ALU.mult, op1=ALU.add),
                                  reads=svb + [st32_b[hd], logf_b[h2]], writes=[st16_b[hd]])
                                V(lambda hd=hd, sv=sv, dec=dec: nc.vector.scalar_tensor_tensor(
                                    out=st32[:, hd, :], in0=st32[:, hd, :], scalar=dec,
                                    in1=sv, op0=ALU.mult, op1=ALU.add),
                                  reads=svb + [st32_b[hd], logf_b[h2]], writes=[st32_b[hd]])
                    stg(4.6)
                    for hl in range(4):
                        obk, obb = ob[hl]
                        A(lambda obk=obk, hl=hl: nc.scalar.copy(out=osb[hl], in_=obk[:]), reads=obb, writes=[osb_b[hl]])
                        A(act(osq[hl], obk[:], AF.Square), reads=obb, writes=[osq_b[hl]])
                    if hg < 3:
                        do_f(hg + 1, bankB)
                    for hl in range(4):
                        hd = hg * 4 + hl
                        ss, ssb = bankB()
                        M(lambda ss=ss, hl=hl: nc.tensor.matmul(ss[:], lhsT=onesb[:], rhs=osq[hl], start=True, stop=True),
                          reads=[osq_b[hl], cbuf], writes=ssb)
                        A(act(oln, ss[:], AF.Ln, bias=EPS, scale=1.0 / 128), reads=ssb, writes=[oln_b])
                        A(act(oln, oln, AF.Exp, scale=-0.5), reads=[oln_b], writes=[oln_b])
                        V(lambda hl=hl: nc.vector.tensor_tensor(out=osb[hl], in0=osb[hl], in1=gs[:, hl, :], op=ALU.mult),
                          reads=[osb_b[hl], gs_b[hl]], writes=[osb_b[hl]])
                        V(lambda hd=hd, hl=hl: nc.vector.scalar_tensor_tensor(
                            out=oT[:, hd, :], in0=osb[hl], scalar=vecs[:, 336 + hd:337 + hd], in1=oln,
                            op0=ALU.mult, op1=ALU.mult),
                          reads=[osb_b[hl], oln_b, vbuf], writes=[oT_b[hd]])
                    rot["A"] = 0
                stg(5)
                resid_update([("ao", j) for j in range(4)], gates[0])
                stg(6)
                mlp(0, 1)

                stg(7)
                norm_dual(2, 3)
                KT_O, VT_O = 0, 2048
                kTo = a16(KT_O, 2048).rearrange("p (g t) -> p g t", t=T)
                vTo = a16(VT_O, 2048).rearrange("p (tb c) -> p tb c", c=512)
                kTo_b = [reg(KT_O + g * 512, 512) for g in range(4)]
                vTo_b = [reg(VT_O + tb * 512, 512) for tb in range(4)]

                def ev_k(g, bk, bb):
                    A(lambda: nc.scalar.copy(out=kTo[:, g, :], in_=bk[:]), reads=bb, writes=[kTo_b[g]])
                proj_fm(("kv", 0), hT, hT_b, ev_k)

                def ev_v(tb, bk, bb):
                    A(lambda: nc.scalar.copy(out=vTo[:, tb, :], in_=bk[:]), reads=bb, writes=[vTo_b[tb]])
                proj_tm(("kv", 1), hT, hT_b, ev_v)
                for g in range(4):
                    S_.dma(ACT, kst[g], Kd[g][:, r0:r0 + T], kTo[:, g, :], reads=[kTo_b[g]], writes=[Kd_b[g]])
                    S_.dma(ACT, vst[g], Vd[g][:, ti * 4:(ti + 1) * 4, :], vTo[:, :, g * 128:(g + 1) * 128],
                           reads=vTo_b, writes=[Vd_b[g]])

                stg(8)
                QT_O = 0
                KG_O = [8192, 12288]
                VG_O = [16384, 20480]
                E_O = [24576, 25600]
                SP_O = [26624 + i * 512 for i in range(4)]
                RB_O = [28672 + i * 512 for i in range(4)]
                W_O = [30720, 31232, 31744, 32256]
                R32_O = 31744
                qT = a16(QT_O, 8192).rearrange("p (h t) -> p h t", t=T)
                qT_b = [reg(QT_O + h * 512, 512) for h in range(16)]
                scl = 128 ** -0.5
                for j4 in range(4):
                    def ev_qq(j, bk, bb, j4=j4):
                        hd = j4 * 4 + j
                        A(act(qT[:, hd, :], bk[:], AF.Identity, scale=scl), reads=bb, writes=[qT_b[hd]])
                    proj_fm(("q", j4), oT, oT_b, ev_qq)
                ntok = r0 + T
                nblk = ntok // 128
                kg = [a16(KG_O[i], 4096) for i in range(2)]
                vg = [a16(VG_O[i], 4096).rearrange("p (b d) -> p b d", d=128) for i in range(2)]
                kg_b = [reg(KG_O[i], 4096) for i in range(2)]
                vg_b = [reg(VG_O[i], 4096) for i in range(2)]
                E32 = [a32(E_O[i], 512) for i in range(2)]
                E_b = [reg(E_O[i], 1024) for i in range(2)]
                SPt = [a16(SP_O[i], 512) for i in range(4)]
                SP_b = [reg(SP_O[i], 512) for i in range(4)]
                RBt = [a16(RB_O[i], 512) for i in range(4)]
                RB_b = [reg(RB_O[i], 512) for i in range(4)]
                Wt = [a16(W_O[i], 512) for i in range(4)]
                W_b = [reg(W_O[i], 512) for i in range(4)]
                msk_bc = mask_s[:].unsqueeze(1).broadcast_to([128, 4, 128])

                def load_kv(g):
                    S_.dma(SP, kgv[g % 2], kg[g % 2][:, 0:ntok], Kd[g][:, 0:ntok], reads=[Kd_b[g]], writes=[kg_b[g % 2]])
                    S_.dma(SP, vgv[g % 2], vg[g % 2][:, 0:nblk, :], Vd[g][:, 0:nblk, :], reads=[Vd_b[g]], writes=[vg_b[g % 2]])
                load_kv(0)
                steps = []
                for g in range(4):
                    for qb in range(4):
                        Q = ti * 4 + qb
                        for kb in range(Q, -1, -1):
                            steps.append((g, qb, kb, Q))
                state = {"zb": {}, "ob": None}

                rbs = {}

                def stageA(n):
                    g, qb, kb, Q = steps[n]
                    zi = 4 + (n % 4)
                    zb, zbb = banks[zi], bankb[zi]
                    state["zb"][n] = (zb, zbb)
                    rhs = qT[:, 4 * g:4 * g + 4, qb * 128:(qb + 1) * 128]
                    M(lambda: nc.tensor.matmul(zb[:], lhsT=kg[g % 2][:, kb * 128:(kb + 1) * 128], rhs=rhs,
                                               start=True, stop=(kb != Q)),
                      reads=[kg_b[g % 2]] + qT_b[4 * g:4 * g + 4], writes=zbb)
                    if kb == Q:
                        M(lambda: nc.tensor.matmul(zb[:], lhsT=identb[:], rhs=mneg[:], start=False, stop=True),
                          reads=[cbuf], writes=zbb)
                    e_ap, e_b = E32[n % 2], E_b[n % 2]
                    A(act(e_ap, zb[:], AF.Exp), reads=zbb, writes=[e_b])

                def stageA2(n):
                    g, qb, kb, Q = steps[n]
                    e_ap, e_b = E32[n % 2], E_b[n % 2]
                    sp_ap, sp_b = SPt[n % 4], SP_b[n % 4]
                    A(act(sp_ap, e_ap, AF.Ln, bias=1.0, scale=1.0), reads=[e_b], writes=[sp_b])
                    if kb > 0:
                        if kb == Q:
                            rbs[n + 1] = (sp_ap, sp_b)
                        else:
                            r_ap, r_b = rbs.pop(n)
                            rbs[n] = (r_ap, r_b)
                            o_ap, o_b = RBt[(n + 1) % 4], RB_b[(n + 1) % 4]
                            V(lambda: nc.vector.tensor_tensor(out=o_ap, in0=r_ap, in1=sp_ap, op=ALU.add),
                              reads=[r_b, sp_b], writes=[o_b])
                            rbs[n + 1] = (o_ap, o_b)

                def stageB(n):
                    g, qb, kb, Q = steps[n]
                    zb, zbb = state["zb"].pop(n)
                    sp_ap, sp_b = SPt[n % 4], SP_b[n % 4]
                    M(lambda: nc.tensor.matmul(zb[:], lhsT=negtri[:], rhs=sp_ap, start=False, stop=(kb == Q),
                                               skip_group_check=True),
                      reads=[sp_b, cbuf], writes=zbb)
                    if kb < Q:
                        r_ap, r_b = rbs.pop(n)
                        M(lambda: nc.tensor.matmul(zb[:], lhsT=negones[:], rhs=r_ap, start=False, stop=True,
                                                   skip_group_check=True),
                          reads=[r_b, cbuf], writes=zbb)
                    w_ap, w_b = Wt[n % 4], W_b[n % 4]
                    A(act(w_ap, zb[:], AF.Exp), reads=zbb, writes=[w_b])

                def stageC(n):
                    g, qb, kb, Q = steps[n]
                    if kb == Q and qb == 0 and g + 1 < 4:
                        load_kv(g + 1)
                    w_ap, w_b = Wt[n % 4], W_b[n % 4]
                    if kb == Q:
                        state["ob"] = bankA()
                    obk, obb = state["ob"]
                    M(lambda: nc.tensor.matmul(obk[:], lhsT=vg[g % 2][:, kb, :], rhs=w_ap, start=(kb == Q), stop=(kb == 0)),
                      reads=[vg_b[g % 2], w_b], writes=obb)
                    if kb == 0:
                        V(lambda: nc.vector.tensor_copy(out=oT[:, 4 * g:4 * g + 4, qb * 128:(qb + 1) * 128],
                                                        in_=obk[:].rearrange("p (h t) -> p h t", t=128)),
                          reads=obb, writes=oT_b[4 * g:4 * g + 4])

                NS = len(steps)
                stageA(0)
                if NS > 1:
                    stageA(1)
                stageA2(0)
                if NS > 2:
                    stageA(2)
                if NS > 1:
                    stageA2(1)
                for n in range(NS):
                    stageB(n)
                    if n + 3 < NS:
                        stageA(n + 3)
                    if n + 2 < NS:
                        stageA2(n + 2)
                    if n >= 1:
                        stageC(n - 1)
                stageC(NS - 1)
                stg(9)
                resid_update([("bo", j) for j in range(4)], gates[3])
                stg(10)
                mlp(1, 4)

                stg(11)
                FO = 0
                fo = a32(FO, 8192).rearrange("p (k t) -> p k t", t=T)
                fo_b = [reg(FO + k * 1024, 1024) for k in range(KC)]

                rmsnorm(5, lambda kc, t_ap, t_b, g_ap: None, fo_b, direct=lambda kc: (fo[:, kc, :], fo_b[kc]))
                OSTG = 16384
                for tb in range(4):
                    sg = a32(OSTG + (tb % 2) * 4096, 2048)
                    sg_b = reg(OSTG + (tb % 2) * 4096, 4096, "ostg")
                    for q in range(4):
                        bk, bb = bankB()
                        for i in range(4):
                            kc = q * 4 + i
                            M(lambda kc=kc, i=i, bk=bk, tb=tb: nc.tensor.transpose(
                                bk[:, i * 128:(i + 1) * 128], fo[:, kc, tb * 128:(tb + 1) * 128], ident32[:]),
                              reads=[fo_b[kc], cbuf], writes=[bb[i]])
                        A(lambda q=q, bk=bk, sg=sg: nc.scalar.copy(out=sg[:, q * 512:(q + 1) * 512], in_=bk[:]),
                          reads=bb, writes=[sg_b])
                    S_.dma(ACT, outv[tb % 2], y_d[r0 + tb * 128:r0 + (tb + 1) * 128, :], sg, reads=[sg_b])
            if not S_.dry:
                nc.scalar.wait_ge(outv[0].sem, outv[0].count)
                nc.scalar.wait_ge(outv[1].sem, outv[1].count)

        S_.dry = True
        program()
        S_.dry = False
        program()
    return nc


def _vec_layout(v):
    return np.ascontiguousarray(np.asarray(v, np.float32).reshape(-1, 128).T)


_NC_CACHE = {}


def kernel(x, c, ada_w, ada_b, norm_mix, norm_mlp, a_w_in, a_lb_logits, a_out_gain, a_w_out,
           kv_ada_w, kv_ada_b, kv_norm, w_kv, b_w_q, b_w_out, mlp_w1, mlp_w2, final_norm,
           _ntiles=S // T, _cores=8, _stage=99):
    f = lambda a: np.ascontiguousarray(np.asarray(a, np.float32))
    x = f(x)
    key = (_ntiles, _stage)
    if key not in _NC_CACHE:
        _NC_CACHE[key] = build(_ntiles, stage=_stage)
    nc = _NC_CACHE[key]
    srcs = {
        "ada_w0": f(ada_w[0]), "ada_w1": f(ada_w[1]), "kv_ada_w": f(kv_ada_w), "w_in": f(a_w_in[0]),
        "a_w_out": f(a_w_out[0]), "w_kv": f(w_kv), "b_w_q": f(b_w_q[0]), "b_w_out": f(b_w_out[0]),
        "w1_0": f(mlp_w1[0]), "w1_1": f(mlp_w1[1]), "w2_0": f(mlp_w2[0]), "w2_1": f(mlp_w2[1]),
    }

    def tile_a(w, col0):
        return w[:, col0:col0 + 512].reshape(16, 128, 512).transpose(1, 0, 2).reshape(128, 8192)

    def tile_b(w, a_, q_):
        return w[a_ * 4096:(a_ + 1) * 4096, q_ * 256:(q_ + 1) * 256].reshape(32, 128, 256).transpose(1, 0, 2).reshape(128, 8192)

    wt32 = np.empty((94, 128, 8192), np.float32)
    for idx, (kind, src, args) in enumerate(_WLIST):
        wt32[idx] = tile_a(srcs[src], args) if kind == "A" else tile_b(srcs[src], *args)
    ada32 = np.empty((56, 128, 8192), np.float32)
    for name, base, nt in (("ada_w0", 0, 24), ("ada_w1", 24, 24), ("kv_ada_w", 48, 8)):
        for j in range(nt):
            ada32[base + j] = tile_a(srcs[name], j * 512)
    shared = {"wt32": wt32, "ada32": ada32}
    in_maps = []
    for b in range(_cores):
        vec = np.concatenate([
            _vec_layout(c[b]), _vec_layout(ada_b[0]), _vec_layout(ada_b[1]), _vec_layout(kv_ada_b),
            _vec_layout(norm_mix[0]), _vec_layout(norm_mix[1]), _vec_layout(norm_mlp[0]), _vec_layout(norm_mlp[1]),
            _vec_layout(a_lb_logits[0]), _vec_layout(a_lb_logits[1]), _vec_layout(a_out_gain[0]),
            _vec_layout(kv_norm), _vec_layout(final_norm)], axis=1)
        assert vec.shape == (128, NVEC)
        m = dict(shared)
        m["x"] = x[b]
        m["vecs"] = np.ascontiguousarray(vec)
        in_maps.append(m)
    res = run_bass_kernel_spmd(nc, in_maps, core_ids=list(range(_cores)))
    out = np.stack([np.asarray(res.results[b]["y"], np.float32) for b in range(_cores)], axis=0)
    return out
```

```python
from contextlib import ExitStack
import numpy as np
import concourse.bass as bass
import concourse.mybir as mybir
from concourse.bass_utils import run_bass_kernel_spmd

F32 = mybir.dt.float32
BF16 = mybir.dt.bfloat16
AF = mybir.ActivationFunctionType
ALU = mybir.AluOpType

D = 2048
S = 4096
T = 512
KC = 16
DFF = 8192
EPS = 1e-6
NVEC = 384
NB = 3
PF = 2
POOL_KC = (1, 3, 6, 8, 11, 13)


class Eng:
    def __init__(self, name, handle, sem, is_pe=False):
        self.name = name
        self.h = handle
        self.sem = sem
        self.count = 0
        self.waited = {}
        self.is_pe = is_pe


class Buf:
    __slots__ = ("w", "r", "name")

    def __init__(self, name=""):
        self.w = None
        self.r = {}
        self.name = name


class Sched:
    def __init__(self, nc, stack):
        self.nc = nc
        self.stack = stack
        self.dry = False
        self.nsem = 0
        self.pe = Eng("pe", nc.tensor, self.sem("pe"), True)
        self.act = Eng("act", nc.scalar, self.sem("act"))
        self.dve = Eng("dve", nc.vector, self.sem("dve"))
        self.pool = Eng("pool", nc.gpsimd, self.sem("pool"))
        self.sp = Eng("sp", nc.sync, self.sem("sp"))
        self.regions = []

    def sem(self, name):
        self.nsem += 1
        return self.stack.enter_context(self.nc.semaphore("sem_" + name))

    def veng(self, name):
        return Eng(name, None, self.sem(name))

    def _deps(self, reads, writes):
        deps = {}

        def add(e, v):
            if deps.get(e, 0) < v:
                deps[e] = v
        for b in reads:
            if b.w is not None:
                add(*b.w)
        for b in writes:
            if b.w is not None:
                add(*b.w)
            for e, v in b.r.items():
                add(e, v)
        return deps

    def _wait(self, eng, deps):
        for e, v in deps.items():
            if e is eng and eng.is_pe:
                continue
            if eng.waited.get(e, 0) >= v:
                continue
            eng.h.wait_ge(e.sem, v)
            eng.waited[e] = v

    def op(self, eng, fn, reads=(), writes=()):
        if self.dry:
            return
        self._wait(eng, self._deps(reads, writes))
        inst = fn()
        eng.count += 1
        inst.then_inc(eng.sem, 1)
        for b in reads:
            b.r[eng] = eng.count
        for b in writes:
            b.w = (eng, eng.count)
            b.r = {}

    def dma(self, q, ve, out, in_, reads=(), writes=()):
        if self.dry:
            return
        deps = self._deps(reads, writes)
        if ve.count > 0 and deps.get(ve, 0) < ve.count:
            deps[ve] = ve.count
        self._wait(q, deps)
        inst = q.h.dma_start(out=out, in_=in_)
        ve.count += 16
        inst.then_inc(ve.sem, 16)
        for b in reads:
            b.r[ve] = ve.count
        for b in writes:
            b.w = (ve, ve.count)
            b.r = {}

    def region(self, off, size, name=""):
        nb = Buf(name)
        if self.dry:
            return nb
        keep = []
        for (o, s, b) in self.regions:
            if o < off + size and off < o + s:
                if b.w is not None:
                    e, v = b.w
                    if nb.r.get(e, 0) < v:
                        nb.r[e] = v
                for e, v in b.r.items():
                    if nb.r.get(e, 0) < v:
                        nb.r[e] = v
                if o >= off and o + s <= off + size:
                    continue
            keep.append((o, s, b))
        keep.append((off, size, nb))
        self.regions = keep
        return nb


class WStream:
    def __init__(self, sched, slots, slotbufs, vengs, specs):
        self.S = sched
        self.slots = slots
        self.bufs = slotbufs
        self.vengs = vengs
        self.specs = specs
        self.i = 0
        self.issued = 0
        self.fences = []

    def fence(self):
        if self.S.dry:
            self.fences.append(len(self.specs))

    def next(self, spec):
        S = self.S
        i = self.i
        self.i += 1
        if S.dry:
            self.specs.append(spec)
            return self.slots[i % NB], self.bufs[i % NB]
        lim = min(i + PF, len(self.specs) - 1)
        for f in self.fences:
            if i < f:
                lim = min(lim, f - 1)
                break
        while self.issued <= lim:
            j = self.issued
            sp_ = self.specs[j](self.slots[j % NB])
            q, out_view, in_ap, rbufs = sp_[:4]
            S.dma(q, self.vengs[j % NB], out_view, in_ap, reads=rbufs, writes=[self.bufs[j % NB]])
            if len(sp_) > 4:
                q2, dst_ap, dst_bufs = sp_[4:]
                S.dma(q2, self.svengs[j % NB], dst_ap, self.slots[j % NB][:], reads=[self.bufs[j % NB]], writes=dst_bufs)
            self.issued += 1
        return self.slots[i % NB], self.bufs[i % NB]


_WLIST = []


class _Stop(Exception):
    pass


def build(ntiles=S // T, dbg=False, stage=99):
    nc = bass.Bass("TRN2", target_bir_lowering=False)
    NTOK = S

    def din(name, shape, dt=F32):
        return nc.dram_tensor(name, shape, dt, kind="ExternalInput").ap()

    x_d = din("x", [S, D])
    vec_d = din("vecs", [128, NVEC])
    wt32_d = din("wt32", [94, 128, 8192])
    ada32_d = din("ada32", [56, 128, 8192])
    adaw_d = ["ada_w0", "ada_w1"]
    kvada_d = "kv_ada_w"
    win_d, wao_d, wkv_d, wq_d, wbo_d = "w_in", "a_w_out", "w_kv", "b_w_q", "b_w_out"
    w1_d = ["w1_0", "w1_1"]
    w2_d = ["w2_0", "w2_1"]
    y_d = nc.dram_tensor("y", [S, D], F32, kind="ExternalOutput").ap()
    NW = 94
    Wd = nc.dram_tensor("Wd", [NW, 128, 8192], BF16, kind="Internal").ap()
    Kd = nc.dram_tensor("Kd", [4, 128, S], BF16, kind="Internal").ap()
    Vd = nc.dram_tensor("Vd", [4, 128, S // 128, 128], BF16, kind="Internal").ap()

    with ExitStack() as st:
        S_ = Sched(nc, st)
        PE, ACT, DVE, POOL, SP = S_.pe, S_.act, S_.dve, S_.pool, S_.sp

        def sb(name, shape, dt):
            return st.enter_context(nc.sbuf_tensor(name, shape, dt))

        xT = sb("xT", [128, KC, T], F32)
        hT = sb("hT", [128, KC, T], BF16)
        oT = sb("oT", [128, KC, T], BF16)
        wsl = [sb("wslot%d" % i, [128, 8192], BF16) for i in range(NB)]
        st32 = sb("st32", [128, 16, 128], F32)
        st16 = sb("st16", [128, 16, 128], BF16)
        vecs = sb("vecs_sb", [128, NVEC], F32)
        mod = [sb("mod0", [128, 96], F32), sb("mod1", [128, 96], F32)]
        modkv = sb("modkv", [128, 32], F32)
        gv = sb("gv", [128, 5, 16], F32)
        lbv = sb("lbv", [128, 3, 16], F32)
        cact = sb("cact", [128, 16], BF16)
        ones32 = sb("ones32", [128, 128], F32)
        ident32 = sb("ident32", [128, 128], F32)
        identb = sb("identb", [128, 128], BF16)
        onesb = sb("onesb", [128, 128], BF16)
        negones = sb("negones", [128, 128], BF16)
        negtri = sb("negtri", [128, 128], BF16)
        mask_h = sb("mask_h", [128, 128], F32)
        mask_s = sb("mask_s", [128, 128], BF16)
        resetm = sb("resetm", [128, T], F32)
        negbig = sb("negbig", [128, 128], BF16)
        mneg = sb("mneg", [128, 4, 128], BF16)
        ARENA = 32768
        arena = sb("arena", [128, ARENA], BF16)

        banks = [st.enter_context(nc.psum_tensor("bank%d" % i, [128, 512], F32)) for i in range(8)]
        bankb = [[Buf("bank%d_%d" % (i, j)) for j in range(4)] for i in range(8)]
        rot = {"A": 0, "B": 0, "Q": 0}

        def bankA():
            i = rot["A"] % 4
            rot["A"] += 1
            return banks[i], bankb[i]

        def bankB():
            i = 4 + rot["B"] % 4
            rot["B"] += 1
            return banks[i], bankb[i]

        def quarterB():
            bk, bb = bankB()
            return bk[:, 0:128], bb

        def a16(off, n):
            return arena[:, off:off + n]

        def a32(off, n):
            return arena[:, off:off + 2 * n].bitcast(F32)

        def reg(off, n, name=""):
            return S_.region(off, n, name)

        xT_b = [Buf("xT%d" % k) for k in range(KC)]
        hT_b = [Buf("hT%d" % k) for k in range(KC)]
        oT_b = [Buf("oT%d" % k) for k in range(KC)]
        st32_b = [Buf("st32_%d" % k) for k in range(16)]
        st16_b = [Buf("st16_%d" % k) for k in range(16)]
        wbufs = [Buf("wslot%d" % i) for i in range(NB)]
        wvengs = [S_.veng("wv%d" % i) for i in range(NB)]
        cbuf = Buf("consts")
        vbuf = Buf("vecs")
        Wd_b = [Buf("Wd%d" % i) for i in range(94)]
        NCV = 8
        cast_slot_b = [Buf("cslot%d" % i) for i in range(NCV)]
        Kd_b = [Buf("Kd%d" % i) for i in range(4)]
        Vd_b = [Buf("Vd%d" % i) for i in range(4)]
        castv = [S_.veng("cast%d" % i) for i in range(8)]
        kst = [S_.veng("kst%d" % i) for i in range(4)]
        vst = [S_.veng("vst%d" % i) for i in range(4)]
        ldv = [S_.veng("ld0"), S_.veng("ld1")]
        kgv = [S_.veng("kg0"), S_.veng("kg1")]
        vgv = [S_.veng("vg0"), S_.veng("vg1")]
        outv = [S_.veng("outv0"), S_.veng("outv1")]

        def A(fn, reads=(), writes=()):
            S_.op(ACT, fn, reads, writes)

        def V(fn, reads=(), writes=()):
            S_.op(DVE, fn, reads, writes)

        def P(fn, reads=(), writes=()):
            S_.op(POOL, fn, reads, writes)

        def M(fn, reads=(), writes=()):
            S_.op(PE, fn, reads, writes)

        def act(out, in_, func, bias=None, scale=None):
            kw = {}
            if bias is not None:
                kw["bias"] = bias
            if scale is not None:
                kw["scale"] = scale
            return lambda: nc.scalar.activation(out=out, in_=in_, func=func, **kw)

        cur_ti = [0]

        def specA(idx):
            first = (cur_ti[0] == 0)

            def f(slot):
                if not first:
                    return SP, slot[:], Wd[idx], [Wd_b[idx]]
                ov = slot.rearrange("p (a e) -> p a e", e=2048)
                iv = wt32_d[idx].rearrange("p (a e) -> p a e", e=2048)
                return POOL, ov, iv, [], SP, Wd[idx], [Wd_b[idx]]
            return f

        ADA_BASE = {"ada_w0": 0, "ada_w1": 24, "kv_ada_w": 48}

        def specAda(src, col0):
            ai = ADA_BASE[src] + col0 // 512

            def f(slot):
                return (POOL, slot.rearrange("p (a e) -> p a e", e=2048),
                        ada32_d[ai].rearrange("p (a e) -> p a e", e=2048), [])
            return f

        wlist = []

        def addA(src, col0):
            wlist.append(("A", src, col0))
            return len(wlist) - 1

        def addB(src, a, q):
            wlist.append(("B", src, (a, q)))
            return len(wlist) - 1

        widx = {}
        for hg in range(4):
            for m in range(4):
                widx[("in", hg, m)] = addA(win_d, m * 2048 + hg * 512)
        for j in range(4):
            widx[("ao", j)] = addA(wao_d, j * 512)
        for l in range(2):
            for a in range(2):
                for j in range(8):
                    widx[("w1", l, a, j)] = addA(w1_d[l], a * 4096 + j * 512)
                for q in range(8):
                    widx[("w2", l, a, q)] = addB(w2_d[l], a, q)
        for j in range(2):
            widx[("kv", j)] = addA(wkv_d, j * 512)
        for j in range(4):
            widx[("q", j)] = addA(wq_d, j * 512)
        for j in range(4):
            widx[("bo", j)] = addA(wbo_d, j * 512)
        assert len(wlist) == NW
        _WLIST[:] = wlist

        ws = WStream(S_, wsl, wbufs, wvengs, [])
        ws.svengs = [S_.veng("wsv%d" % i) for i in range(NB)]

        def stg(n):
            if stage < n:
                raise _Stop()

        def program():
            try:
                program_()
            except _Stop:
                pass

        def program_():
            rot["A"] = rot["B"] = rot["Q"] = 0
            ws.i = 0
            ws.issued = 0
            P(lambda: nc.gpsimd.memset(ones32[:], 1.0), writes=[cbuf])
            P(lambda: nc.gpsimd.memset(onesb[:], 1.0), writes=[cbuf])
            P(lambda: nc.gpsimd.memset(negones[:], -1.0), writes=[cbuf])
            P(lambda: nc.gpsimd.memset(resetm[:], 1.0), writes=[cbuf])
            P(lambda: nc.gpsimd.memset(resetm[:].rearrange("p (c t) -> p c t", t=64)[:, :, 0:1], 0.0),
              reads=[cbuf], writes=[cbuf])
            P(lambda: nc.gpsimd.memset(st32[:], 0.0), writes=st32_b)
            P(lambda: nc.gpsimd.memset(st16[:], 0.0), writes=st16_b)
            P(lambda: nc.gpsimd.affine_select(out=ident32[:], in_=ones32[:], pattern=[[1, 128]],
                                              compare_op=ALU.is_equal, fill=0.0, base=0, channel_multiplier=-1),
              reads=[cbuf], writes=[cbuf])
            P(lambda: nc.gpsimd.tensor_copy(out=identb[:], in_=ident32[:]), reads=[cbuf], writes=[cbuf])
            P(lambda: nc.gpsimd.affine_select(out=mask_h[:], in_=ones32[:], pattern=[[1, 128]],
                                              compare_op=ALU.is_ge, fill=0.0, base=0, channel_multiplier=-1),
              reads=[cbuf], writes=[cbuf])
            P(lambda: nc.gpsimd.memset(mask_h[0:64, 64:128], 0.0), reads=[cbuf], writes=[cbuf])
            P(lambda: nc.gpsimd.affine_select(out=mask_s[:], in_=ones32[:], pattern=[[1, 128]],
                                              compare_op=ALU.is_ge, fill=0.0, base=-1, channel_multiplier=-1),
              reads=[cbuf], writes=[cbuf])
            P(lambda: nc.gpsimd.affine_select(out=negtri[:], in_=negones[:], pattern=[[-1, 128]],
                                              compare_op=ALU.is_ge, fill=0.0, base=0, channel_multiplier=1),
              reads=[cbuf], writes=[cbuf])
            P(lambda: nc.gpsimd.memset(negbig[:], -30000.0), writes=[cbuf])
            for h_ in range(4):
                P(lambda h_=h_: nc.gpsimd.affine_select(out=mneg[:, h_, :], in_=negbig[:], pattern=[[-1, 128]],
                                                  compare_op=ALU.is_ge, fill=0.0, base=0, channel_multiplier=1),
                  reads=[cbuf], writes=[cbuf])
            S_.dma(SP, ldv[0], vecs[:], vec_d, writes=[vbuf])
            A(act(cact[:], vecs[:, 0:16], AF.Silu), reads=[vbuf], writes=[cbuf])
            V(lambda: nc.vector.tensor_tensor(out=lbv[:, 0, :], in0=vecs[:, 304:320], in1=vecs[:, 320:336],
                                              op=ALU.subtract), reads=[vbuf], writes=[cbuf])
            A(act(lbv[:, 0, :], lbv[:, 0, :], AF.Sigmoid), reads=[cbuf], writes=[cbuf])
            V(lambda: nc.vector.tensor_scalar(out=lbv[:, 1, :], in0=lbv[:, 0, :], scalar1=-1.0, scalar2=1.0,
                                              op0=ALU.mult, op1=ALU.add), reads=[cbuf], writes=[cbuf])
            V(lambda: nc.vector.tensor_scalar(out=lbv[:, 2, :], in0=lbv[:, 0, :], scalar1=1.0, scalar2=-1.0,
                                              op0=ALU.mult, op1=ALU.add), reads=[cbuf], writes=[cbuf])

            stg(1)
            def ada(src, ncol, dst, bias_cols):
                mps, mpb = banks[4], bankb[4]
                for j in range(ncol // 512):
                    wt, wb = ws.next(specAda(src, j * 512))
                    wv = wt.rearrange("p (kc c) -> p kc c", c=512)
                    for jj in range(4):
                        c = j * 4 + jj
                        for kc in range(KC):
                            M(lambda kc=kc, jj=jj, c=c: nc.tensor.matmul(
                                mps[:, c:c + 1], lhsT=wv[:, kc, jj * 128:(jj + 1) * 128], rhs=cact[:, kc:kc + 1],
                                start=(kc == 0), stop=(kc == KC - 1)), reads=[wb, cbuf], writes=mpb)
                nco = ncol // 128
                V(lambda: nc.vector.tensor_tensor(out=dst[:, 0:nco], in0=mps[:, 0:nco], in1=bias_cols, op=ALU.add),
                  reads=mpb + [vbuf], writes=[cbuf])

            ada(adaw_d[0], 6 * D, mod[0], vecs[:, 16:112])
            ada(adaw_d[1], 6 * D, mod[1], vecs[:, 112:208])
            ada(kvada_d, 2 * D, modkv, vecs[:, 208:240])
            ws.fence()
            gsrc = [(mod[0][:, 16:32], vecs[:, 240:256]), (mod[0][:, 64:80], vecs[:, 272:288]),
                    (modkv[:, 16:32], vecs[:, 352:368]), (mod[1][:, 16:32], vecs[:, 256:272]),
                    (mod[1][:, 64:80], vecs[:, 288:304])]
            for gi, (sc, gn) in enumerate(gsrc):
                V(lambda gi=gi, sc=sc, gn=gn: nc.vector.scalar_tensor_tensor(
                    out=gv[:, gi, :], in0=sc, scalar=1.0, in1=gn, op0=ALU.add, op1=ALU.mult),
                  reads=[cbuf, vbuf], writes=[cbuf])
            shs = [mod[0][:, 0:16], mod[0][:, 48:64], modkv[:, 0:16], mod[1][:, 0:16], mod[1][:, 48:64]]
            gates = [mod[0][:, 32:48], mod[0][:, 80:96], None, mod[1][:, 32:48], mod[1][:, 80:96]]

            stg(2)
            NRM = 20480

            use_pool = [False]

            def rmsnorm(gi, out_fn, out_bufs, direct=None):
                sq = a16(NRM, 8192).rearrange("p (k t) -> p k t", t=T)
                sq_b = [reg(NRM + k * 512, 512, "sq") for k in range(KC)]
                lnb = a32(NRM + 8192, 512)
                rstd = a32(NRM + 9216, 512)
                ln_b = reg(NRM + 8192, 1024, "ln")
                rs_b = reg(NRM + 9216, 1024, "rstd")
                tmp = [a32(NRM + 10240 + i * 1024, 512) for i in range(2)] + [a32(NRM - 2048 + i * 1024, 512) for i in range(2)]
                tmp_b = [reg(NRM + 10240 + i * 1024, 1024, "ntmp") for i in range(2)] + \
                        [reg(NRM - 2048 + i * 1024, 1024, "ntmp") for i in range(2)]
                for kc in range(KC):
                    A(act(sq[:, kc, :], xT[:, kc, :], AF.Square), reads=[xT_b[kc]], writes=[sq_b[kc]])
                ss, ssb = bankB()
                for kc in range(KC):
                    M(lambda kc=kc: nc.tensor.matmul(ss[:], lhsT=onesb[:], rhs=sq[:, kc, :], start=(kc == 0),
                                                     stop=(kc == KC - 1)), reads=[sq_b[kc], cbuf], writes=ssb)
                A(act(lnb, ss[:], AF.Ln, bias=EPS, scale=1.0 / D), reads=ssb, writes=[ln_b])
                A(act(rstd, lnb, AF.Exp, scale=-0.5), reads=[ln_b], writes=[rs_b])
                for kc in range(KC):
                    g_ap = (gv[:, gi, kc:kc + 1] if gi < 5 else vecs[:, 368 + kc:369 + kc])
                    if direct is None and kc % 16 in POOL_KC and use_pool[0]:
                        t_ap, t_b = tmp[2 + (kc % 2)], tmp_b[2 + (kc % 2)]
                        P(lambda kc=kc, t_ap=t_ap: nc.gpsimd.tensor_tensor(out=t_ap, in0=xT[:, kc, :], in1=rstd, op=ALU.mult),
                          reads=[xT_b[kc], rs_b], writes=[t_b])
                        out_fn(kc, t_ap, t_b, g_ap)
                        continue
                    t_ap, t_b = tmp[kc % 2], tmp_b[kc % 2]
                    if direct is not None:
                        t_ap, t_b = direct(kc)
                    V(lambda kc=kc, t_ap=t_ap: nc.vector.scalar_tensor_tensor(
                        out=t_ap, in0=xT[:, kc, :], scalar=g_ap,
                        in1=rstd, op0=ALU.mult, op1=ALU.mult), reads=[xT_b[kc], rs_b, cbuf, vbuf], writes=[t_b])
                    out_fn(kc, t_ap, t_b, None)

            def norm_to_hT(gi):
                def fin(kc, t_ap, t_b, g_ap):
                    A(act(hT[:, kc, :], t_ap, AF.Identity, bias=shs[gi][:, kc:kc + 1], scale=(1.0 if g_ap is None else g_ap)),
                      reads=[t_b, cbuf], writes=[hT_b[kc]])
                rmsnorm(gi, fin, hT_b)

            def norm_dual(gi_a, gi_b):
                sq = a16(NRM, 8192).rearrange("p (k t) -> p k t", t=T)
                sq_b = [reg(NRM + k * 512, 512, "sq") for k in range(KC)]
                lnb = a32(NRM + 8192, 512)
                rstd = a32(NRM + 9216, 512)
                ln_b = reg(NRM + 8192, 1024, "ln")
                rs_b = reg(NRM + 9216, 1024, "rstd")
                T16 = 4096
                t16 = a32(T16, 8192).rearrange("p (k t) -> p k t", t=T)
                t16_b = [reg(T16 + k * 1024, 1024, "t16") for k in range(KC)]
                for kc in range(KC):
                    A(act(sq[:, kc, :], xT[:, kc, :], AF.Square), reads=[xT_b[kc]], writes=[sq_b[kc]])
                ss, ssb = bankB()
                for kc in range(KC):
                    M(lambda kc=kc: nc.tensor.matmul(ss[:], lhsT=onesb[:], rhs=sq[:, kc, :], start=(kc == 0),
                                                     stop=(kc == KC - 1)), reads=[sq_b[kc], cbuf], writes=ssb)
                A(act(lnb, ss[:], AF.Ln, bias=EPS, scale=1.0 / D), reads=ssb, writes=[ln_b])
                A(act(rstd, lnb, AF.Exp, scale=-0.5), reads=[ln_b], writes=[rs_b])
                for kc in range(KC):
                    if kc % 16 in POOL_KC and use_pool[0]:
                        P(lambda kc=kc: nc.gpsimd.tensor_tensor(out=t16[:, kc, :], in0=xT[:, kc, :], in1=rstd, op=ALU.mult),
                          reads=[xT_b[kc], rs_b], writes=[t16_b[kc]])
                    else:
                        V(lambda kc=kc: nc.vector.tensor_tensor(out=t16[:, kc, :], in0=xT[:, kc, :], in1=rstd, op=ALU.mult),
                          reads=[xT_b[kc], rs_b], writes=[t16_b[kc]])
                for gi, dst, dst_b in ((gi_a, hT, hT_b), (gi_b, oT, oT_b)):
                    for kc in range(KC):
                        A(act(dst[:, kc, :], t16[:, kc, :], AF.Identity, bias=shs[gi][:, kc:kc + 1],
                              scale=gv[:, gi, kc:kc + 1]), reads=[t16_b[kc], cbuf], writes=[dst_b[kc]])

            def proj_fm(widx_key, src, src_b, evac, bank_fn=None):
                wt, wb = ws.next(specA(widx[widx_key]))
                wv = wt.rearrange("p (kc c) -> p kc c", c=512)
                for j in range(4):
                    bk, bb = (bank_fn or bankA)()
                    for kc in range(KC):
                        M(lambda kc=kc, j=j, bk=bk: nc.tensor.matmul(
                            bk[:], lhsT=wv[:, kc, j * 128:(j + 1) * 128], rhs=src[:, kc, :],
                            start=(kc == 0), stop=(kc == KC - 1)), reads=[wb, src_b[kc]], writes=bb)
                    evac(j, bk, bb)

            def proj_tm(widx_key, src, src_b, evac):
                wt, wb = ws.next(specA(widx[widx_key]))
                wv = wt.rearrange("p (kc c) -> p kc c", c=512)
                for tb in range(4):
                    bk, bb = bankA()
                    for kc in range(KC):
                        M(lambda kc=kc, tb=tb, bk=bk: nc.tensor.matmul(
                            bk[:], lhsT=src[:, kc, tb * 128:(tb + 1) * 128], rhs=wv[:, kc, :],
                            start=(kc == 0), stop=(kc == KC - 1)), reads=[wb, src_b[kc]], writes=bb)
                    evac(tb, bk, bb)

            def resid_update(widx_keys, gate):
                for j4, key in enumerate(widx_keys):
                    def ev(j, bk, bb, j4=j4):
                        dc = j4 * 4 + j
                        V(lambda: nc.vector.scalar_tensor_tensor(
                            out=xT[:, dc, :], in0=bk[:], scalar=gate[:, dc:dc + 1], in1=xT[:, dc, :],
                            op0=ALU.mult, op1=ALU.add), reads=bb + [xT_b[dc], cbuf], writes=[xT_b[dc]])
                    proj_fm(key, oT, oT_b, ev)

            def mlp(l, gi):
                norm_to_hT(gi)
                gate = gates[gi]
                HID = 0
                hid = a16(HID, 16384).rearrange("p (k t) -> p k t", t=T)
                rl = [a16(16384 + i * 512, 512) for i in range(2)]
                for a in range(2):
                    hid_b = [reg(HID + k * 512, 512, "hid") for k in range(32)]
                    rl_b = [reg(16384 + i * 512, 512, "rl") for i in range(2)]
                    cnt = [0]
                    for j8 in range(8):
                        def ev(j, bk, bb, j8=j8):
                            hc = j8 * 4 + j
                            r_ap, r_b = rl[cnt[0] % 2], rl_b[cnt[0] % 2]
                            cnt[0] += 1
                            A(act(r_ap, bk[:], AF.Relu), reads=bb, writes=[r_b])
                            V(lambda: nc.vector.tensor_tensor(out=hid[:, hc, :], in0=r_ap, in1=r_ap, op=ALU.mult),
                              reads=[r_b], writes=[hid_b[hc]])
                        proj_fm(("w1", l, a, j8), hT, hT_b, ev)
                    for q in range(8):
                        wt, wb = ws.next(specA(widx[("w2", l, a, q)]))
                        wv = wt.rearrange("p (hc c) -> p hc c", c=256)
                        for dd in range(2):
                            dc = q * 2 + dd
                            bk, bb = bankA()
                            for hc in range(32):
                                M(lambda hc=hc, dd=dd, bk=bk: nc.tensor.matmul(
                                    bk[:], lhsT=wv[:, hc, dd * 128:(dd + 1) * 128], rhs=hid[:, hc, :],
                                    start=(hc == 0), stop=(hc == 31)), reads=[wb, hid_b[hc]], writes=bb)
                            V(lambda dc=dc, bk=bk: nc.vector.scalar_tensor_tensor(
                                out=xT[:, dc, :], in0=bk[:], scalar=gate[:, dc:dc + 1], in1=xT[:, dc, :],
                                op0=ALU.mult, op1=ALU.add), reads=bb + [xT_b[dc], cbuf], writes=[xT_b[dc]])

            stg(3)
            for ti in range(ntiles):
                r0 = ti * T
                cur_ti[0] = ti
                use_pool[0] = ti >= 1
                STG = 0
                for tb in range(4):
                    sg = a32(STG + (tb % 2) * 4096, 2048)
                    sg_b = reg(STG + (tb % 2) * 4096, 4096, "stg")
                    S_.dma(SP, ldv[tb % 2], sg, x_d[r0 + tb * 128:r0 + (tb + 1) * 128, :], writes=[sg_b])
                    for q in range(4):
                        bk, bb = bankB()
                        for i in range(4):
                            kc = q * 4 + i
                            M(lambda kc=kc, i=i, bk=bk: nc.tensor.transpose(
                                bk[:, i * 128:(i + 1) * 128], sg[:, kc * 128:(kc + 1) * 128], ident32[:]),
                              reads=[sg_b, cbuf], writes=[bb[i]])
                        if q % 2 == 0:
                            A(lambda q=q, tb=tb, bk=bk: nc.scalar.copy(
                                out=xT[:, q * 4:(q + 1) * 4, tb * 128:(tb + 1) * 128],
                                in_=bk[:].rearrange("p (i t) -> p i t", t=128)),
                              reads=bb, writes=[xT_b[q * 4 + i] for i in range(4)])
                        else:
                            V(lambda q=q, tb=tb, bk=bk: nc.vector.tensor_copy(
                                out=xT[:, q * 4:(q + 1) * 4, tb * 128:(tb + 1) * 128],
                                in_=bk[:].rearrange("p (i t) -> p i t", t=128)),
                              reads=bb, writes=[xT_b[q * 4 + i] for i in range(4)])

                stg(4)
                norm_to_hT(0)
                stg(4.1)
                O_SIG, O_LOGF, O_CUM = 0, 4096, 8192
                O_QS, O_QD, O_KI, O_KS, O_KT, O_VT, O_GS = 12288, 14336, 16384, 18432, 20480, 22528, 24576
                O_TMP = 26624
                for hg in range(4):
                    if hg == 0:
                        def v4(off, dt32):
                            if dt32:
                                return a32(off, 2048).rearrange("p (h t) -> p h t", t=T)
                            return a16(off, 2048).rearrange("p (h t) -> p h t", t=T)
                        sig, logf, cum = v4(O_SIG, True), v4(O_LOGF, True), v4(O_CUM, True)
                        qs, qd, ki, ks, gs = v4(O_QS, False), v4(O_QD, False), v4(O_KI, False), v4(O_KS, False), v4(O_GS, False)
                        kt = a16(O_KT, 2048).rearrange("p (tb h k) -> p tb h k", h=4, k=128)
                        vt = a16(O_VT, 2048).rearrange("p (tb c) -> p tb c", c=512)
                        sig_b = [reg(O_SIG + h * 1024, 1024) for h in range(4)]
                        logf_b = [reg(O_LOGF + h * 1024, 1024) for h in range(4)]
                        cum_b = [reg(O_CUM + h * 1024, 1024) for h in range(4)]
                        qs_b = [reg(O_QS + h * 512, 512) for h in range(4)]
                        qd_b = [reg(O_QD + h * 512, 512) for h in range(4)]
                        ki_b = [reg(O_KI + h * 512, 512) for h in range(4)]
                        ks_b = [reg(O_KS + h * 512, 512) for h in range(4)]
                        gs_b = [reg(O_GS + h * 512, 512) for h in range(4)]
                        kt_b = [reg(O_KT + tb * 512, 512) for tb in range(4)]
                        vt_b = [reg(O_VT + tb * 512, 512) for tb in range(4)]
                        sct = [a16(O_TMP + i * 128, 128) for i in range(4)]
                        sct_b = [reg(O_TMP + i * 128, 128) for i in range(4)]
                        osb = [a16(O_TMP + 512 + i * 512, 512) for i in range(4)]
                        osb_b = [reg(O_TMP + 512 + i * 512, 512) for i in range(4)]
                        osq = [a16(O_TMP + 2560 + i * 512, 512) for i in range(4)]
                        osq_b = [reg(O_TMP + 2560 + i * 512, 512) for i in range(4)]
                        oln = a32(O_TMP + 4608, 512)
                        oln_b = reg(O_TMP + 4608, 1024)

                    def ev_f(hl, bk, bb, hg_=None):
                        hd = hg_ * 4 + hl
                        A(act(sig[:, hl, :], bk[:], AF.Sigmoid), reads=bb, writes=[sig_b[hl]])
                        A(act(logf[:, hl, :], sig[:, hl, :], AF.Ln, bias=lbv[:, 0, hd:hd + 1], scale=lbv[:, 1, hd:hd + 1]),
                          reads=[sig_b[hl], cbuf], writes=[logf_b[hl]])
                        V(lambda: nc.vector.tensor_scalar(out=sig[:, hl, :], in0=sig[:, hl, :],
                                                          scalar1=lbv[:, 2, hd:hd + 1], scalar2=lbv[:, 1, hd:hd + 1],
                                                          op0=ALU.mult, op1=ALU.add),
                          reads=[sig_b[hl], cbuf], writes=[sig_b[hl]])
                        V(lambda: nc.vector.tensor_tensor_scan(out=cum[:, hl, :], data0=resetm[:], data1=logf[:, hl, :],
                                                               initial=0.0, op0=ALU.mult, op1=ALU.add),
                          reads=[logf_b[hl], cbuf], writes=[cum_b[hl]])
                        A(act(logf[:, hl, :], cum[:, hl, :], AF.Exp), reads=[cum_b[hl]], writes=[logf_b[hl]])
                        A(act(cum[:, hl, :], cum[:, hl, :], AF.Exp, scale=-1.0), reads=[cum_b[hl]], writes=[cum_b[hl]])
                        V(lambda: nc.vector.tensor_tensor(out=ki[:, hl, :], in0=sig[:, hl, :], in1=cum[:, hl, :], op=ALU.mult),
                          reads=[sig_b[hl], cum_b[hl]], writes=[ki_b[hl]])
                        decb = logf[:, hl, :].rearrange("p (c t) -> p c t", t=64)[:, :, 63:64].broadcast_to([128, 8, 64])
                        V(lambda: nc.vector.tensor_tensor(out=ks[:, hl, :].rearrange("p (c t) -> p c t", t=64),
                                                          in0=ki[:, hl, :].rearrange("p (c t) -> p c t", t=64),
                                                          in1=decb, op=ALU.mult),
                          reads=[ki_b[hl], logf_b[hl]], writes=[ks_b[hl]])
                    def do_f(hg_, bank_fn=None):
                        proj_fm(("in", hg_, 1), hT, hT_b, lambda hl, bk, bb: ev_f(hl, bk, bb, hg_), bank_fn=bank_fn)
                    if hg == 0:
                        do_f(0)
                    stg(4.2)

                    def ev_q(hl, bk, bb):
                        A(act(qs[:, hl, :], bk[:], AF.Sigmoid), reads=bb, writes=[qs_b[hl]])
                        V(lambda: nc.vector.tensor_tensor(out=qs[:, hl, :], in0=bk[:], in1=qs[:, hl, :], op=ALU.mult),
                          reads=bb + [qs_b[hl]], writes=[qs_b[hl]])
                        V(lambda: nc.vector.tensor_tensor(out=qd[:, hl, :], in0=qs[:, hl, :], in1=logf[:, hl, :], op=ALU.mult),
                          reads=[qs_b[hl], logf_b[hl]], writes=[qd_b[hl]])
                    proj_fm(("in", hg, 0), hT, hT_b, ev_q)
                    stg(4.3)

                    stg(4.4)

                    def ev_i(tb, bk, bb):
                        A(lambda: nc.scalar.copy(out=vt[:, tb, :], in_=bk[:]), reads=bb, writes=[vt_b[tb]])
                    proj_tm(("in", hg, 2), hT, hT_b, ev_i)
                    stg(4.45)

                    for tb in range(4):
                        bk, bb = bankB()
                        bkb = bk[:].bitcast(BF16)
                        for hl in range(4):
                            M(lambda hl=hl, tb=tb, bkb=bkb: nc.tensor.transpose(
                                bkb[:, hl * 128:(hl + 1) * 128], ks[:, hl, tb * 128:(tb + 1) * 128], identb[:]),
                              reads=[ks_b[hl], cbuf], writes=bb[0:2])
                        A(lambda tb=tb, bkb=bkb: nc.scalar.copy(
                            out=kt[:, tb, :, :], in_=bkb[:, 0:512].rearrange("p (h k) -> p h k", k=128)),
                          reads=bb[0:2], writes=[kt_b[tb]])


                    def ev_g(hl, bk, bb):
                        if stage == 4.47:
                            return
                        A(act(gs[:, hl, :], bk[:], AF.Sigmoid), reads=bb, writes=[gs_b[hl]])
                        V(lambda: nc.vector.tensor_tensor(out=gs[:, hl, :], in0=bk[:], in1=gs[:, hl, :], op=ALU.mult),
                          reads=bb + [gs_b[hl]], writes=[gs_b[hl]])
                    proj_fm(("in", hg, 3), hT, hT_b, ev_g)
                    stg(4.5)
                    ob = [(banks[i], bankb[i]) for i in range(4)]
                    for tb in range(4):
                        scs = []
                        for hl in range(4):
                            qv, qb_ = quarterB()
                            M(lambda hl=hl, tb=tb, qv=qv: nc.tensor.matmul(
                                qv, lhsT=ki[:, hl, tb * 128:(tb + 1) * 128], rhs=qd[:, hl, tb * 128:(tb + 1) * 128],
                                start=True, stop=True), reads=[ki_b[hl], qd_b[hl]], writes=qb_)
                            s_ap, s_b = sct[hl], sct_b[hl]
                            V(lambda qv=qv, s_ap=s_ap: nc.vector.tensor_tensor(out=s_ap, in0=qv, in1=mask_h[:], op=ALU.mult),
                              reads=qb_ + [cbuf], writes=[s_b])
                            scs.append((s_ap, s_b))
                        for h2 in range(4):
                            s_ap2, s_b2 = scs[h2]
                            obk, obb = ob[h2]
                            M(lambda h2=h2, tb=tb, obk=obk, s_ap2=s_ap2: nc.tensor.matmul(
                                obk[:, tb * 128:(tb + 1) * 128], lhsT=vt[:, tb, h2 * 128:(h2 + 1) * 128], rhs=s_ap2,
                                start=True, stop=False), reads=[vt_b[tb], s_b2], writes=obb)
                        for cc in range(2):
                            c = tb * 2 + cc
                            for h2 in range(4):
                                hd = hg * 4 + h2
                                obk, obb = ob[h2]
                                M(lambda h2=h2, c=c, hd=hd, obk=obk, cc=cc: nc.tensor.matmul(
                                    obk[:, c * 64:(c + 1) * 64], lhsT=st16[:, hd, :], rhs=qd[:, h2, c * 64:(c + 1) * 64],
                                    start=False, stop=(cc == 1)), reads=[st16_b[hd], qd_b[h2]], writes=obb)
                                sv, svb = quarterB()
                                M(lambda h2=h2, tb=tb, cc=cc, sv=sv: nc.tensor.matmul(
                                    sv, lhsT=kt[cc * 64:(cc + 1) * 64, tb, h2, :],
                                    rhs=vt[cc * 64:(cc + 1) * 64, tb, h2 * 128:(h2 + 1) * 128],
                                    start=True, stop=True), reads=[kt_b[tb], vt_b[tb]], writes=svb)
                                dec = logf[:, h2, c * 64 + 63:c * 64 + 64]
                                V(lambda hd=hd, sv=sv, dec=dec: nc.vector.scalar_tensor_tensor(
                                    out=st16[:, hd, :], in0=st32[:, hd, :], scalar=dec,
                                    in1=sv, op0=ALU.mult, op1=ALU.add),
                                  reads=svb + [st32_b[hd], logf_b[h2]], writes=[st16_b[hd]])
                                V(lambda hd=hd, sv=sv, dec=dec: nc.vector.scalar_tensor_tensor(
                                    out=st32[:, hd, :], in0=st32[:, hd, :], scalar=dec,
                                    in1=sv, op0=ALU.mult, op1=ALU.add),
                                  reads=svb + [st32_b[hd], logf_b[h2]], writes=[st32_b[hd]])
                    stg(4.6)
                    for hl in range(4):
                        obk, obb = ob[hl]
                        A(lambda obk=obk, hl=hl: nc.scalar.copy(out=osb[hl], in_=obk[:]), reads=obb, writes=[osb_b[hl]])
                        A(act(osq[hl], obk[:], AF.Square), reads=obb, writes=[osq_b[hl]])
                    if hg < 3:
                        do_f(hg + 1, bankB)
                    for hl in range(4):
                        hd = hg * 4 + hl
                        ss, ssb = bankB()
                        M(lambda ss=ss, hl=hl: nc.tensor.matmul(ss[:], lhsT=onesb[:], rhs=osq[hl], start=True, stop=True),
                          reads=[osq_b[hl], cbuf], writes=ssb)
                        A(act(oln, ss[:], AF.Ln, bias=EPS, scale=1.0 / 128), reads=ssb, writes=[oln_b])
                        A(act(oln, oln, AF.Exp, scale=-0.5), reads=[oln_b], writes=[oln_b])
                        V(lambda hl=hl: nc.vector.tensor_tensor(out=osb[hl], in0=osb[hl], in1=gs[:, hl, :], op=ALU.mult),
                          reads=[osb_b[hl], gs_b[hl]], writes=[osb_b[hl]])
                        V(lambda hd=hd, hl=hl: nc.vector.scalar_tensor_tensor(
                            out=oT[:, hd, :], in0=osb[hl], scalar=vecs[:, 336 + hd:337 + hd], in1=oln,
                            op0=ALU.mult, op1=ALU.mult),
                          reads=[osb_b[hl], oln_b, vbuf], writes=[oT_b[hd]])
                    rot["A"] = 0
                stg(5)
                resid_update([("ao", j) for j in range(4)], gates[0])
                stg(6)
                mlp(0, 1)

                stg(7)
                norm_dual(2, 3)
                KT_O, VT_O = 0, 2048
                kTo = a16(KT_O, 2048).rearrange("p (g t) -> p g t", t=T)
                vTo = a16(VT_O, 2048).rearrange("p (tb c) -> p tb c", c=512)
                kTo_b = [reg(KT_O + g * 512, 512) for g in range(4)]
                vTo_b = [reg(VT_O + tb * 512, 512) for tb in range(4)]

                def ev_k(g, bk, bb):
                    A(lambda: nc.scalar.copy(out=kTo[:, g, :], in_=bk[:]), reads=bb, writes=[kTo_b[g]])
                proj_fm(("kv", 0), hT, hT_b, ev_k)

                def ev_v(tb, bk, bb):
                    A(lambda: nc.scalar.copy(out=vTo[:, tb, :], in_=bk[:]), reads=bb, writes=[vTo_b[tb]])
                proj_tm(("kv", 1), hT, hT_b, ev_v)
                for g in range(4):
                    S_.dma(ACT, kst[g], Kd[g][:, r0:r0 + T], kTo[:, g, :], reads=[kTo_b[g]], writes=[Kd_b[g]])
                    S_.dma(ACT, vst[g], Vd[g][:, ti * 4:(ti + 1) * 4, :], vTo[:, :, g * 128:(g + 1) * 128],
                           reads=vTo_b, writes=[Vd_b[g]])

                stg(8)
                QT_O = 0
                KG_O = [8192, 12288]
                VG_O = [16384, 20480]
                E_O = [24576, 25600]
                SP_O = [26624 + i * 512 for i in range(4)]
                RB_O = [28672 + i * 512 for i in range(4)]
                W_O = [30720, 31232, 31744, 32256]
                R32_O = 31744
                qT = a16(QT_O, 8192).rearrange("p (h t) -> p h t", t=T)
                qT_b = [reg(QT_O + h * 512, 512) for h in range(16)]
                scl = 128 ** -0.5
                for j4 in range(4):
                    def ev_qq(j, bk, bb, j4=j4):
                        hd = j4 * 4 + j
                        A(act(qT[:, hd, :], bk[:], AF.Identity, scale=scl), reads=bb, writes=[qT_b[hd]])
                    proj_fm(("q", j4), oT, oT_b, ev_qq)
                ntok = r0 + T
                nblk = ntok // 128
                kg = [a16(KG_O[i], 4096) for i in range(2)]
                vg = [a16(VG_O[i], 4096).rearrange("p (b d) -> p b d", d=128) for i in range(2)]
                kg_b = [reg(KG_O[i], 4096) for i in range(2)]
                vg_b = [reg(VG_O[i], 4096) for i in range(2)]
                E32 = [a32(E_O[i], 512) for i in range(2)]
                E_b = [reg(E_O[i], 1024) for i in range(2)]
                SPt = [a16(SP_O[i], 512) for i in range(4)]
                SP_b = [reg(SP_O[i], 512) for i in range(4)]
                RBt = [a16(RB_O[i], 512) for i in range(4)]
                RB_b = [reg(RB_O[i], 512) for i in range(4)]
                Wt = [a16(W_O[i], 512) for i in range(4)]
                W_b = [reg(W_O[i], 512) for i in range(4)]
                msk_bc = mask_s[:].unsqueeze(1).broadcast_to([128, 4, 128])

                def load_kv(g):
                    S_.dma(SP, kgv[g % 2], kg[g % 2][:, 0:ntok], Kd[g][:, 0:ntok], reads=[Kd_b[g]], writes=[kg_b[g % 2]])
                    S_.dma(SP, vgv[g % 2], vg[g % 2][:, 0:nblk, :], Vd[g][:, 0:nblk, :], reads=[Vd_b[g]], writes=[vg_b[g % 2]])
                load_kv(0)
                steps = []
                for g in range(4):
                    for qb in range(4):
                        Q = ti * 4 + qb
                        for kb in range(Q, -1, -1):
                            steps.append((g, qb, kb, Q))
                state = {"zb": {}, "ob": None}

                rbs = {}

                def stageA(n):
                    g, qb, kb, Q = steps[n]
                    zi = 4 + (n % 4)
                    zb, zbb = banks[zi], bankb[zi]
                    state["zb"][n] = (zb, zbb)
                    rhs = qT[:, 4 * g:4 * g + 4, qb * 128:(qb + 1) * 128]
                    M(lambda: nc.tensor.matmul(zb[:], lhsT=kg[g % 2][:, kb * 128:(kb + 1) * 128], rhs=rhs,
                                               start=True, stop=(kb != Q)),
                      reads=[kg_b[g % 2]] + qT_b[4 * g:4 * g + 4], writes=zbb)
                    if kb == Q:
                        M(lambda: nc.tensor.matmul(zb[:], lhsT=identb[:], rhs=mneg[:], start=False, stop=True),
                          reads=[cbuf], writes=zbb)
                    e_ap, e_b = E32[n % 2], E_b[n % 2]
                    A(act(e_ap, zb[:], AF.Exp), reads=zbb, writes=[e_b])

                def stageA2(n):
                    g, qb, kb, Q = steps[n]
                    e_ap, e_b = E32[n % 2], E_b[n % 2]
                    sp_ap, sp_b = SPt[n % 4], SP_b[n % 4]
                    A(act(sp_ap, e_ap, AF.Ln, bias=1.0, scale=1.0), reads=[e_b], writes=[sp_b])
                    if kb > 0:
                        if kb == Q:
                            rbs[n + 1] = (sp_ap, sp_b)
                        else:
                            r_ap, r_b = rbs.pop(n)
                            rbs[n] = (r_ap, r_b)
                            o_ap, o_b = RBt[(n + 1) % 4], RB_b[(n + 1) % 4]
                            V(lambda: nc.vector.tensor_tensor(out=o_ap, in0=r_ap, in1=sp_ap, op=ALU.add),
                              reads=[r_b, sp_b], writes=[o_b])
                            rbs[n + 1] = (o_ap, o_b)

                def stageB(n):
                    g, qb, kb, Q = steps[n]
                    zb, zbb = state["zb"].pop(n)
                    sp_ap, sp_b = SPt[n % 4], SP_b[n % 4]
                    M(lambda: nc.tensor.matmul(zb[:], lhsT=negtri[:], rhs=sp_ap, start=False, stop=(kb == Q),
                                               skip_group_check=True),
                      reads=[sp_b, cbuf], writes=zbb)
                    if kb < Q:
                        r_ap, r_b = rbs.pop(n)
                        M(lambda: nc.tensor.matmul(zb[:], lhsT=negones[:], rhs=r_ap, start=False, stop=True,
                                                   skip_group_check=True),
                          reads=[r_b, cbuf], writes=zbb)
                    w_ap, w_b = Wt[n % 4], W_b[n % 4]
                    A(act(w_ap, zb[:], AF.Exp), reads=zbb, writes=[w_b])

                def stageC(n):
                    g, qb, kb, Q = steps[n]
                    if kb == Q and qb == 0 and g + 1 < 4:
                        load_kv(g + 1)
                    w_ap, w_b = Wt[n % 4], W_b[n % 4]
                    if kb == Q:
                        state["ob"] = bankA()
                    obk, obb = state["ob"]
                    M(lambda: nc.tensor.matmul(obk[:], lhsT=vg[g % 2][:, kb, :], rhs=w_ap, start=(kb == Q), stop=(kb == 0)),
                      reads=[vg_b[g % 2], w_b], writes=obb)
                    if kb == 0:
                        V(lambda: nc.vector.tensor_copy(out=oT[:, 4 * g:4 * g + 4, qb * 128:(qb + 1) * 128],
                                                        in_=obk[:].rearrange("p (h t) -> p h t", t=128)),
                          reads=obb, writes=oT_b[4 * g:4 * g + 4])

                NS = len(steps)
                stageA(0)
                if NS > 1:
                    stageA(1)
                stageA2(0)
                if NS > 2:
                    stageA(2)
                if NS > 1:
                    stageA2(1)
                for n in range(NS):
                    stageB(n)
                    if n + 3 < NS:
                        stageA(n + 3)
                    if n + 2 < NS:
                        stageA2(n + 2)
                    if n >= 1:
                        stageC(n - 1)
                stageC(NS - 1)
                stg(9)
                resid_update([("bo", j) for j in range(4)], gates[3])
                stg(10)
                mlp(1, 4)

                stg(11)
                FO = 0
                fo = a32(FO, 8192).rearrange("p (k t) -> p k t", t=T)
                fo_b = [reg(FO + k * 1024, 1024) for k in range(KC)]

                rmsnorm(5, lambda kc, t_ap, t_b, g_ap: None, fo_b, direct=lambda kc: (fo[:, kc, :], fo_b[kc]))
                OSTG = 16384
                for tb in range(4):
                    sg = a32(OSTG + (tb % 2) * 4096, 2048)
                    sg_b = reg(OSTG + (tb % 2) * 4096, 4096, "ostg")
                    for q in range(4):
                        bk, bb = bankB()
                        for i in range(4):
                            kc = q * 4 + i
                            M(lambda kc=kc, i=i, bk=bk, tb=tb: nc.tensor.transpose(
                                bk[:, i * 128:(i + 1) * 128], fo[:, kc, tb * 128:(tb + 1) * 128], ident32[:]),
                              reads=[fo_b[kc], cbuf], writes=[bb[i]])
                        if q % 2 == 0:
                            A(lambda q=q, bk=bk, sg=sg: nc.scalar.copy(out=sg[:, q * 512:(q + 1) * 512], in_=bk[:]),
                              reads=bb, writes=[sg_b])
                        else:
                            V(lambda q=q, bk=bk, sg=sg: nc.vector.tensor_copy(out=sg[:, q * 512:(q + 1) * 512], in_=bk[:]),
                              reads=bb, writes=[sg_b])
                    S_.dma(ACT, outv[tb % 2], y_d[r0 + tb * 128:r0 + (tb + 1) * 128, :], sg, reads=[sg_b])
            if not S_.dry:
                nc.scalar.wait_ge(outv[0].sem, outv[0].count)
                nc.scalar.wait_ge(outv[1].sem, outv[1].count)

        S_.dry = True
        program()
        S_.dry = False
        program()
    return nc


def _vec_layout(v):
    return np.ascontiguousarray(np.asarray(v, np.float32).reshape(-1, 128).T)


_NC_CACHE = {}


def kernel(x, c, ada_w, ada_b, norm_mix, norm_mlp, a_w_in, a_lb_logits, a_out_gain, a_w_out,
           kv_ada_w, kv_ada_b, kv_norm, w_kv, b_w_q, b_w_out, mlp_w1, mlp_w2, final_norm,
           _ntiles=S // T, _cores=8, _stage=99):
    f = lambda a: np.ascontiguousarray(np.asarray(a, np.float32))
    x = f(x)
    key = (_ntiles, _stage)
    if key not in _NC_CACHE:
        _NC_CACHE[key] = build(_ntiles, stage=_stage)
    nc = _NC_CACHE[key]
    srcs = {
        "ada_w0": f(ada_w[0]), "ada_w1": f(ada_w[1]), "kv_ada_w": f(kv_ada_w), "w_in": f(a_w_in[0]),
        "a_w_out": f(a_w_out[0]), "w_kv": f(w_kv), "b_w_q": f(b_w_q[0]), "b_w_out": f(b_w_out[0]),
        "w1_0": f(mlp_w1[0]), "w1_1": f(mlp_w1[1]), "w2_0": f(mlp_w2[0]), "w2_1": f(mlp_w2[1]),
    }

    def tile_a(w, col0):
        return w[:, col0:col0 + 512].reshape(16, 128, 512).transpose(1, 0, 2).reshape(128, 8192)

    def tile_b(w, a_, q_):
        return w[a_ * 4096:(a_ + 1) * 4096, q_ * 256:(q_ + 1) * 256].reshape(32, 128, 256).transpose(1, 0, 2).reshape(128, 8192)

    wt32 = np.empty((94, 128, 8192), np.float32)
    for idx, (kind, src, args) in enumerate(_WLIST):
        wt32[idx] = tile_a(srcs[src], args) if kind == "A" else tile_b(srcs[src], *args)
    ada32 = np.empty((56, 128, 8192), np.float32)
    for name, base, nt in (("ada_w0", 0, 24), ("ada_w1", 24, 24), ("kv_ada_w", 48, 8)):
        for j in range(nt):
            ada32[base + j] = tile_a(srcs[name], j * 512)
    shared = {"wt32": wt32, "ada32": ada32}
    in_maps = []
    for b in range(_cores):
        vec = np.concatenate([
            _vec_layout(c[b]), _vec_layout(ada_b[0]), _vec_layout(ada_b[1]), _vec_layout(kv_ada_b),
            _vec_layout(norm_mix[0]), _vec_layout(norm_mix[1]), _vec_layout(norm_mlp[0]), _vec_layout(norm_mlp[1]),
            _vec_layout(a_lb_logits[0]), _vec_layout(a_lb_logits[1]), _vec_layout(a_out_gain[0]),
            _vec_layout(kv_norm), _vec_layout(final_norm)], axis=1)
        assert vec.shape == (128, NVEC)
        m = dict(shared)
        m["x"] = x[b]
        m["vecs"] = np.ascontiguousarray(vec)
        in_maps.append(m)
    res = run_bass_kernel_spmd(nc, in_maps, core_ids=list(range(_cores)))
    out = np.stack([np.asarray(res.results[b]["y"], np.float32) for b in range(_cores)], axis=0)
    return out
```
